# Optimizing a Trainium2 kernel written in Bass

```python
import jax, jax.numpy as jnp
from jax import lax
import numpy as np

D_MODEL = 4096
BATCH = 2
SEQ = 8192
DEPTH = 1

HEAD_DIM = 128
MOBA_HEADS = 16
MOBA_WIDTH = MOBA_HEADS * HEAD_DIM
MOBA_BLOCK = 256
MOBA_TOPK = 3
MOBA_QCHUNK = 32
SGU_CHUNK = 128
SGU_WIDTH = D_MODEL // 2
SGU_GROUPS = 16
SGU_GROUP_DIM = SGU_WIDTH // SGU_GROUPS
N_MEM = 256
XATTN_HEADS = 4
XATTN_WIDTH = XATTN_HEADS * HEAD_DIM
D_FF = -(-8 * D_MODEL // (3 * 256)) * 256
IN_COLS = 3 * MOBA_WIDTH + 2 * SGU_WIDTH + 2 * D_MODEL
RMS_EPS = 1e-6
LN_EPS = 1e-5
ROPE_THETA = 10000.0
NEG_INF = -1e30

kernel_name = 'moba_sgu_gated_hybrid_block'


def rms_norm(x, g):
    xf = x.astype(jnp.float32)
    y = xf * lax.rsqrt(jnp.mean(xf * xf, axis=-1, keepdims=True) + RMS_EPS)
    return (y * g.astype(jnp.float32)).astype(x.dtype)


def layer_norm(x, g, b):
    xf = x.astype(jnp.float32)
    mu = jnp.mean(xf, axis=-1, keepdims=True)
    var = jnp.mean(jnp.square(xf - mu), axis=-1, keepdims=True)
    y = (xf - mu) * lax.rsqrt(var + LN_EPS)
    return (y * g.astype(jnp.float32) + b.astype(jnp.float32)).astype(x.dtype)


def rotary(x, pos):
    dh = x.shape[-1]
    half = dh // 2
    inv_freq = jnp.power(ROPE_THETA, -(jnp.arange(half, dtype=jnp.float32) * 2.0 / dh))
    ang = pos.astype(jnp.float32)[:, None] * inv_freq[None, :]
    cos, sin = jnp.cos(ang), jnp.sin(ang)
    xf = x.astype(jnp.float32)
    x1, x2 = xf[..., :half], xf[..., half:]
    return jnp.concatenate([x1 * cos - x2 * sin, x2 * cos + x1 * sin], axis=-1).astype(x.dtype)


def moba_attention(q, k, v):
    bsz, n_heads, seq, dh = q.shape
    pad = (-seq) % MOBA_BLOCK
    padw = ((0, 0), (0, 0), (0, pad), (0, 0))
    q = jnp.pad(q, padw)
    k = jnp.pad(k, padw)
    v = jnp.pad(v, padw)
    s_pad = seq + pad
    n_blocks = s_pad // MOBA_BLOCK
    top_k = min(MOBA_TOPK, n_blocks)
    n_sel = top_k * MOBA_BLOCK
    kb = k.reshape(bsz, n_heads, n_blocks, MOBA_BLOCK, dh)
    vb = v.reshape(bsz, n_heads, n_blocks, MOBA_BLOCK, dh)
    k_mean = jnp.mean(kb.astype(jnp.float32), axis=3)
    scale = dh ** -0.5
    b_idx = jnp.arange(bsz)[:, None, None, None]
    h_idx = jnp.arange(n_heads)[None, :, None, None]

    def one_chunk(ci):
        start = ci * MOBA_QCHUNK
        cur = start // MOBA_BLOCK
        qc = lax.dynamic_slice_in_dim(q, start, MOBA_QCHUNK, axis=2).astype(jnp.float32)
        gate = jnp.einsum('bhqd,bhnd->bhqn', qc, k_mean)
        gate = jnp.where(jnp.arange(n_blocks) < cur, gate, NEG_INF)
        _, sel = lax.top_k(gate, top_k)
        valid = jnp.arange(top_k) < cur
        k_sel = kb[b_idx, h_idx, sel].astype(jnp.float32)
        v_sel = vb[b_idx, h_idx, sel].astype(jnp.float32)
        s_sel = jnp.einsum('bhqd,bhqnkd->bhqnk', qc, k_sel) * scale
        s_sel = jnp.where(valid[:, None], s_sel, NEG_INF)
        k_own = lax.dynamic_slice_in_dim(k, cur * MOBA_BLOCK, MOBA_BLOCK, axis=2).astype(jnp.float32)
        v_own = lax.dynamic_slice_in_dim(v, cur * MOBA_BLOCK, MOBA_BLOCK, axis=2).astype(jnp.float32)
        s_own = jnp.einsum('bhqd,bhkd->bhqk', qc, k_own) * scale
        q_pos = start + jnp.arange(MOBA_QCHUNK)
        k_pos = cur * MOBA_BLOCK + jnp.arange(MOBA_BLOCK)
        s_own = jnp.where(k_pos[None, :] <= q_pos[:, None], s_own, NEG_INF)
        scores = jnp.concatenate([s_sel.reshape(bsz, n_heads, MOBA_QCHUNK, n_sel), s_own], axis=-1)
        p = jax.nn.softmax(scores, axis=-1)
        p_sel = p[..., :n_sel].reshape(bsz, n_heads, MOBA_QCHUNK, top_k, MOBA_BLOCK)
        out = (jnp.einsum('bhqnk,bhqnkd->bhqd', p_sel, v_sel)
               + jnp.einsum('bhqk,bhkd->bhqd', p[..., n_sel:], v_own))
        return out.astype(v.dtype)

    outs = lax.map(one_chunk, jnp.arange(s_pad // MOBA_QCHUNK))
    out = jnp.transpose(outs, (1, 2, 0, 3, 4)).reshape(bsz, n_heads, s_pad, dh)
    return out[:, :, :seq]


def spatial_gating(u, v, ln_g, ln_b, w_s, b_s):
    bsz, seq, _ = v.shape
    v = layer_norm(v, ln_g, ln_b)
    vc = v.reshape(bsz, seq // SGU_CHUNK, SGU_CHUNK, SGU_GROUPS, SGU_GROUP_DIM)
    causal = jnp.tril(jnp.ones((SGU_CHUNK, SGU_CHUNK), dtype=w_s.dtype))
    w = w_s * causal[None]
    mixed = jnp.einsum('gts,bcsgd->bctgd', w, vc) + jnp.transpose(b_s)[None, None, :, :, None]
    return u * mixed.reshape(bsz, seq, SGU_WIDTH)


def hybrid_mixer(h, w_in, sgu_ln_g, sgu_ln_b, w_sgu, b_sgu, w_branch_a, w_branch_b, w_out):
    bsz, seq, _ = h.shape
    z = h @ w_in
    cuts = np.cumsum([MOBA_WIDTH, MOBA_WIDTH, MOBA_WIDTH, SGU_WIDTH, SGU_WIDTH, D_MODEL]).tolist()
    q, k, v, u, vs, g_a, g_b = jnp.split(z, cuts, axis=-1)

    def heads(t):
        return jnp.transpose(t.reshape(bsz, seq, MOBA_HEADS, HEAD_DIM), (0, 2, 1, 3))

    pos = jnp.arange(seq)
    o_a = moba_attention(rotary(heads(q), pos), rotary(heads(k), pos), heads(v))
    o_a = jnp.transpose(o_a, (0, 2, 1, 3)).reshape(bsz, seq, MOBA_WIDTH)
    o_b = spatial_gating(jax.nn.gelu(u), jax.nn.gelu(vs), sgu_ln_g, sgu_ln_b, w_sgu, b_sgu)
    merged = jax.nn.sigmoid(g_a) * (o_a @ w_branch_a) + jax.nn.sigmoid(g_b) * (o_b @ w_branch_b)
    return merged @ w_out


def memory_cross_attention(h, mem_n, w_xq, w_xkv, w_xo):
    bsz, seq, _ = h.shape
    n_mem = mem_n.shape[1]
    q = (h @ w_xq).reshape(bsz, seq, XATTN_HEADS, HEAD_DIM)
    k, v = jnp.split(mem_n @ w_xkv, 2, axis=-1)
    k = k.reshape(bsz, n_mem, XATTN_HEADS, HEAD_DIM)
    v = v.reshape(bsz, n_mem, XATTN_HEADS, HEAD_DIM)
    s = jnp.einsum('bshd,bmhd->bhsm', q, k).astype(jnp.float32) * (HEAD_DIM ** -0.5)
    p = jax.nn.softmax(s, axis=-1).astype(v.dtype)
    o = jnp.einsum('bhsm,bmhd->bshd', p, v).reshape(bsz, seq, XATTN_WIDTH)
    return o @ w_xo


def swiglu(h, w_gate, w_up, w_down):
    return (jax.nn.silu(h @ w_gate) * (h @ w_up)) @ w_down


def setup_inputs(seed: int = 0) -> dict:
    key = jax.random.key(seed)
    ks = jax.random.split(key, 24)
    f32 = jnp.float32

    def nrm(k, shape, scale):
        return jax.random.normal(k, shape, f32) * scale

    def gain(k, shape):
        return 1.0 + 0.05 * jax.random.normal(k, shape, f32)

    L = DEPTH
    return {
        'x': nrm(ks[0], (BATCH, SEQ, D_MODEL), 1.0),
        'mem': nrm(ks[1], (BATCH, N_MEM, D_MODEL), 1.0),
        'norm_mix_g': gain(ks[2], (L, D_MODEL)),
        'w_in': nrm(ks[3], (L, D_MODEL, IN_COLS), D_MODEL ** -0.5),
        'sgu_ln_g': gain(ks[4], (L, SGU_WIDTH)),
        'sgu_ln_b': nrm(ks[5], (L, SGU_WIDTH), 0.02),
        'w_sgu': nrm(ks[6], (L, SGU_GROUPS, SGU_CHUNK, SGU_CHUNK), SGU_CHUNK ** -0.5),
        'b_sgu': gain(ks[7], (L, SGU_GROUPS, SGU_CHUNK)),
        'w_branch_a': nrm(ks[8], (L, MOBA_WIDTH, D_MODEL), MOBA_WIDTH ** -0.5),
        'w_branch_b': nrm(ks[9], (L, SGU_WIDTH, D_MODEL), SGU_WIDTH ** -0.5),
        'w_out': nrm(ks[10], (L, D_MODEL, D_MODEL), D_MODEL ** -0.5),
        'norm_xattn_g': gain(ks[11], (L, D_MODEL)),
        'norm_mem_g': gain(ks[12], (L, D_MODEL)),
        'w_xq': nrm(ks[13], (L, D_MODEL, XATTN_WIDTH), D_MODEL ** -0.5),
        'w_xkv': nrm(ks[14], (L, D_MODEL, 2 * XATTN_WIDTH), D_MODEL ** -0.5),
        'w_xo': nrm(ks[15], (L, XATTN_WIDTH, D_MODEL), XATTN_WIDTH ** -0.5),
        'norm_ffn_g': gain(ks[16], (L, D_MODEL)),
        'w_ff_gate': nrm(ks[17], (L, D_MODEL, D_FF), D_MODEL ** -0.5),
        'w_ff_up': nrm(ks[18], (L, D_MODEL, D_FF), D_MODEL ** -0.5),
        'w_ff_down': nrm(ks[19], (L, D_FF, D_MODEL), D_FF ** -0.5),
        'norm_final_g': gain(ks[20], (D_MODEL,)),
    }


def reference(x, mem, norm_mix_g, w_in, sgu_ln_g, sgu_ln_b, w_sgu, b_sgu, w_branch_a,
              w_branch_b, w_out, norm_xattn_g, norm_mem_g, w_xq, w_xkv, w_xo, norm_ffn_g,
              w_ff_gate, w_ff_up, w_ff_down, norm_final_g):
    for l in range(DEPTH):
        h = rms_norm(x, norm_mix_g[l])
        x = x + hybrid_mixer(h, w_in[l], sgu_ln_g[l], sgu_ln_b[l], w_sgu[l], b_sgu[l],
                             w_branch_a[l], w_branch_b[l], w_out[l])
        h = rms_norm(x, norm_xattn_g[l])
        mem_n = rms_norm(mem, norm_mem_g[l])
        x = x + memory_cross_attention(h, mem_n, w_xq[l], w_xkv[l], w_xo[l])
        h = rms_norm(x, norm_ffn_g[l])
        x = x + swiglu(h, w_ff_gate[l], w_ff_up[l], w_ff_down[l])
    return rms_norm(x, norm_final_g)
```

```python
import math
from contextlib import ExitStack

import numpy as np
import concourse.bass as bass
import concourse.mybir as mybir
from concourse.bass_utils import run_bass_kernel_spmd

F32 = mybir.dt.float32
BF16 = mybir.dt.bfloat16
AF = mybir.ActivationFunctionType
ALU = mybir.AluOpType
AX = mybir.AxisListType

D = 4096
KC = D // 128
SEQ = 8192
NB = 2
HEADS = 16
BLK = 256
NBLK = SEQ // BLK
OWN = 2048
T = 512
NT_OWN = OWN // T
NT_ALL = SEQ // T
NMEM = 256
XH = 4
DFF = 11008
FC = DFF // 128
IN_COLS = 18432
C_Q, C_K, C_V, C_U, C_VS, C_GA, C_GB = 0, 2048, 4096, 6144, 8192, 10240, 14336
RMS_EPS = 1e-6
LN_EPS = 1e-5
SCALE = 1.0 / math.sqrt(128.0)
NEG = -1e30


class SemC:
    def __init__(self, h):
        self.h = h
        self.v = 0


class Buf:
    def __init__(self, name="", dram=False, psum=False):
        self.name = name
        self.dram = dram
        self.psum = psum
        self.w = None
        self.r = {}
        self.lsem = None
        self.ssem = None


class Eng:
    def __init__(self, eng, semc, name):
        self.e = eng
        self.s = semc
        self.name = name
        self.seen = {}

    def wait(self, tok):
        s, v = tok
        if self.seen.get(id(s), 0) < v:
            self.e.wait_ge(s.h, v)
            self.seen[id(s)] = v


class K:
    def __init__(self, nc):
        self.nc = nc
        self.top = ExitStack()
        self.free_sems = []
        for i in range(96):
            self.free_sems.append(SemC(self.top.enter_context(nc.semaphore(f"s{i}"))))
        self.PE = Eng(nc.tensor, self.free_sems.pop(), "pe")
        self.ACT = Eng(nc.scalar, self.free_sems.pop(), "act")
        self.DVE = Eng(nc.vector, self.free_sems.pop(), "dve")
        self.POOL = Eng(nc.gpsimd, self.free_sems.pop(), "pool")
        self.SP = Eng(nc.sync, self.free_sems.pop(), "sp")
        self.engs = [self.PE, self.ACT, self.DVE, self.POOL, self.SP]
        self.dma_toks = {}
        self.phase_sems = []
        self.phase_sems_pool = []
        self.free_sems_pool = [self.free_sems.pop() for _ in range(16)]
        self.uid = 0

    def sb(self, st, shape, dt, name=None):
        self.uid += 1
        t = st.enter_context(self.nc.sbuf_tensor(f"{name or 'sb'}_{self.uid}", list(shape), dt))
        b = Buf(name)
        return t, b

    def ps(self, st, shape, dt, name=None):
        self.uid += 1
        t = st.enter_context(self.nc.psum_tensor(f"{name or 'ps'}_{self.uid}", list(shape), dt))
        return t, Buf(name, psum=True)

    def getsem(self, Q):
        if Q is self.POOL:
            s = self.free_sems_pool.pop()
            self.phase_sems_pool.append(s)
        else:
            s = self.free_sems.pop()
            self.phase_sems.append(s)
        return s

    def end_phase(self):
        self.barrier()
        self.free_sems.extend(self.phase_sems)
        self.phase_sems = []
        self.free_sems_pool.extend(self.phase_sems_pool)
        self.phase_sems_pool = []

    def barrier(self):
        toks = [(e.s, e.s.v) for e in self.engs] + list(self.dma_toks.values())
        for e in self.engs:
            for tk in toks:
                if tk[0] is e.s:
                    continue
                e.wait(tk)
        self.dma_toks = {}

    def op(self, E, fn, reads=(), writes=(), inc=True):
        toks = []
        for b in reads:
            if b.w is not None:
                toks.append(b.w)
            if b.psum:
                toks.extend(tk for tk in b.r.values() if tk[0] is not E.s)
        for b in writes:
            if b.w is not None:
                toks.append(b.w)
            toks.extend(b.r.values())
        for tk in toks:
            if E is self.PE and tk[0] is E.s:
                continue
            E.wait(tk)
        ins = fn()
        if inc:
            E.s.v += 1
            ins.then_inc(E.s.h, 1)
            tok = (E.s, E.s.v)
        else:
            tok = (E.s, E.s.v + 1)
        for b in writes:
            b.w = tok
            b.r = {}
        for b in reads:
            b.r[id(tok[0])] = tok
        return ins

    def dma(self, Q, out, in_, sembuf, kind, reads=(), writes=()):
        if kind == "l":
            if sembuf.lsem is None:
                sembuf.lsem = self.getsem(Q)
            s = sembuf.lsem
        else:
            if sembuf.ssem is None:
                sembuf.ssem = self.getsem(Q)
            s = sembuf.ssem
        reads = [b for b in reads if not b.dram]
        writes = [b for b in writes if not b.dram]
        toks = []
        for b in reads:
            if b.w is not None:
                toks.append(b.w)
        for b in writes:
            if b.w is not None and b.w[0] is not s:
                toks.append(b.w)
            toks.extend(b.r.values())
        for tk in toks:
            Q.wait(tk)
        Q.e.dma_start(out=out, in_=in_).then_inc(s.h, 16)
        s.v += 16
        tok = (s, s.v)
        self.dma_toks[id(s)] = tok
        for b in writes:
            b.w = tok
            b.r = {}
        for b in reads:
            b.r[id(s)] = tok
        return tok


class WStream:
    def __init__(self, k, st, nslots, slot_elems, live=1):
        self.k = k
        self.slots = [k.sb(st, [128, slot_elems], BF16, "wslot") for _ in range(nslots)]
        self.reqs = []
        self.issued = 0
        self.taken = 0
        self.depth = nslots - live

    def plan(self, reqs):
        self.reqs = self.reqs + list(reqs)

    def _issue(self, i):
        ap, k0, nk, c0, ncols = self.reqs[i]
        t, b = self.slots[i % len(self.slots)]
        dst = t[:, 0:nk * ncols].rearrange("p (k c) -> p k c", k=nk)
        step = 8
        for ka in range(0, nk, step):
            kb = min(nk, ka + step)
            src = ap[(k0 + ka) * 128:(k0 + kb) * 128, c0:c0 + ncols].rearrange("(k p) c -> p k c", p=128)
            self.k.dma(self.k.POOL, dst[:, ka:kb, :], src, b, "l", writes=[b])

    def next(self):
        i = self.taken
        while self.issued <= min(i + self.depth, len(self.reqs) - 1):
            self._issue(self.issued)
            self.issued += 1
        self.taken += 1
        ap, k0, nk, c0, ncols = self.reqs[i]
        t, b = self.slots[i % len(self.slots)]
        return t[:, 0:nk * ncols].rearrange("p (k c) -> p k c", k=nk), b


class WRes:
    def __init__(self, ws, n):
        self.tiles = [ws.next() for _ in range(n)]
        self.i = 0

    def next(self):
        t = self.tiles[self.i % len(self.tiles)]
        self.i += 1
        return t


def build_program(stop_after=None, dbg=(), skip=()):
    nc = bass.Bass("TRN2", target_bir_lowering=False)
    phases_done = []

    def din(name, shape, dt=F32):
        return nc.dram_tensor(name, list(shape), dt, kind="ExternalInput").ap()

    def dscr(name, shape, dt):
        return nc.dram_tensor(name, list(shape), dt, kind=("ExternalOutput" if name in dbg else "Internal")).ap()

    x_own = din("x_own", [OWN, D])
    x_all = din("x_all", [SEQ, D])
    mem_b = din("mem_b", [NMEM, D])
    g_mix = din("g_mix", [1, D])
    g_xat = din("g_xat", [1, D])
    g_mem = din("g_mem", [1, D])
    g_ffn = din("g_ffn", [1, D])
    g_fin = din("g_fin", [1, D])
    w_in = din("w_in", [D, IN_COLS])
    w_pa = din("w_pa", [2048, D])
    w_pb = din("w_pb", [2048, D])
    w_out = din("w_out", [D, D])
    w_xq = din("w_xq", [D, 512])
    w_xkv = din("w_xkv", [D, 1024])
    w_xo = din("w_xo", [512, D])
    w_fg = din("w_fg", [D, DFF])
    w_fu = din("w_fu", [D, DFF])
    w_fd = din("w_fd", [DFF, D])
    ln_g = din("ln_g", [1, 2048])
    ln_b = din("ln_b", [1, 2048])
    w_sT = din("w_sT", [128, 16 * 128])
    b_s = din("b_s", [1, 16 * 128])
    tri = din("tri", [128, 128])
    ident = din("ident", [128, 128])
    perm = din("perm", [128, 128])
    cos_all = din("cos_all", [128, SEQ])
    sin_all = din("sin_all", [128, SEQ])
    cos_own = din("cos_own", [128, OWN])
    sin_own = din("sin_own", [128, OWN])
    negb = din("negb", [1, 8 * 32])
    validm = din("validm", [1, 8 * 32])
    ownm = din("ownm", [1, 8 * 32])
    dmask = din("dmask", [128, 4 * 512])
    out = nc.dram_tensor("out", [OWN, D], F32, kind="ExternalOutput").ap()

    hT_all = dscr("hT_all", [NT_ALL, 128, KC * T], BF16)
    hT_own = dscr("hT_own", [NT_OWN, 128, KC * T], BF16)
    KT_scr = dscr("KT_scr", [HEADS, 128, SEQ], BF16)
    V_scr = dscr("V_scr", [HEADS, 128, 64 * 128], BF16)
    QT_scr = dscr("QT_scr", [NT_OWN, 128, HEADS * T], BF16)
    obT_scr = dscr("obT_scr", [NT_OWN, 128, 16 * T], BF16)
    oaT_scr = dscr("oaT_scr", [NT_OWN, 128, 16 * T], BF16)
    x1_scr = dscr("x1_scr", [OWN, D], F32)
    x2_scr = dscr("x2_scr", [OWN, D], F32)
    x3_scr = dscr("x3_scr", [OWN, D], F32)
    h2T_scr = dscr("h2T_scr", [NT_OWN, 128, KC * T], BF16)
    h3T_scr = dscr("h3T_scr", [NT_OWN, 128, KC * T], BF16)
    memT_scr = dscr("memT_scr", [1, 128, KC * NMEM], BF16)

    k = K(nc)

    def pbc(a):
        r = a[0].partition_broadcast(128)
        assert len(r.shape) == 2, r.shape
        return r
    PE, ACT, DVE, POOL, SP = k.PE, k.ACT, k.DVE, k.POOL, k.SP

    b_hT_all = [Buf(dram=True) for _ in range(NT_ALL)]
    b_hT_own = [Buf(dram=True) for _ in range(NT_OWN)]
    b_KT = Buf(dram=True)
    b_V = Buf(dram=True)
    b_QT = [Buf(dram=True) for _ in range(NT_OWN)]
    b_obT = [Buf(dram=True) for _ in range(NT_OWN)]
    b_oaT = [Buf(dram=True) for _ in range(NT_OWN)]
    b_x1 = [Buf(dram=True) for _ in range(NT_OWN)]
    b_x2 = [Buf(dram=True) for _ in range(NT_OWN)]
    b_x3 = [Buf(dram=True) for _ in range(NT_OWN)]
    b_h2T = [Buf(dram=True) for _ in range(NT_OWN)]
    b_h3T = [Buf(dram=True) for _ in range(NT_OWN)]
    b_memT = [Buf(dram=True)]

    kmT_f, b_kmf = k.sb(k.top, [128, HEADS, NBLK], F32, "kmf")
    kmT_b, b_kmb = k.sb(k.top, [128, HEADS, NBLK], BF16, "kmb")
    identb, b_id = k.sb(k.top, [128, 128], BF16, "ident")
    permb, b_pm = k.sb(k.top, [128, 128], BF16, "perm")
    k.dma(POOL, identb[:], ident, b_id, "l", writes=[b_id])
    k.dma(POOL, permb[:], perm, b_pm, "l", writes=[b_pm])

    rr = {"ps": 0}

    def phase(name):
        if name in skip:
            return
        with ExitStack() as st_:
            yield st_

    def norm_phase(src, n_tok, gain, dst_scr, dst_bufs, src_bufs=None, tile_tok=T, final_out=None):
        with ExitStack() as st:
            gB, b_g = k.sb(st, [128, D], F32, "gB")
            k.dma(SP, gB[:], pbc(gain), b_g, "l", writes=[b_g])
            xs = [k.sb(st, [128, D], F32, "xs") for _ in range(3)]
            junk, b_junk = k.sb(st, [128, D], BF16, "junk")
            stat = [k.sb(st, [128, 2], F32, "stat") for _ in range(3)]
            if final_out is None:
                hb = [k.sb(st, [128, D], BF16, "hb") for _ in range(2)]
                hTs = [[k.sb(st, [128, 8, tile_tok], BF16, "hTs") for _ in range(4)] for _ in range(2)]
                psT = [k.ps(st, [128, 1024], BF16, "psT") for _ in range(4)]
            else:
                ob = [k.sb(st, [128, D], F32, "ob") for _ in range(2)]
            nsub = n_tok // 128
            spt = tile_tok // 128
            cnt = 0
            for s in range(nsub):
                ti = s // spt
                so = (s % spt) * 128
                xt, xb = xs[s % 3]
                rd = [src_bufs[(s * 128) // T]] if src_bufs is not None else []
                k.dma(SP, xt[:], src[s * 128:(s + 1) * 128, :], xb, "l", reads=rd, writes=[xb])
                stt, stb = stat[s % 3]
                k.op(ACT, lambda: nc.scalar.activation(out=junk[:], in_=xt[:], func=AF.Square,
                                                       accum_out=stt[:, 0:1]),
                     reads=[xb], writes=[b_junk, stb])
                k.op(DVE, lambda: nc.vector.tensor_scalar(out=stt[:, 1:2], in0=stt[:, 0:1], scalar1=1.0 / D,
                                                          scalar2=RMS_EPS, op0=ALU.mult, op1=ALU.add),
                     reads=[stb], writes=[stb])
                k.op(ACT, lambda: nc.scalar.activation(out=stt[:, 1:2], in_=stt[:, 1:2], func=AF.Sqrt),
                     reads=[stb], writes=[stb])
                k.op(DVE, lambda: nc.vector.reciprocal(out=stt[:, 1:2], in_=stt[:, 1:2]),
                     reads=[stb], writes=[stb])
                if final_out is not None:
                    ot, obb = ob[s % 2]
                    k.op(DVE, lambda: nc.vector.scalar_tensor_tensor(out=ot[:], in0=xt[:], scalar=stt[:, 1:2],
                                                                     in1=gB[:], op0=ALU.mult, op1=ALU.mult),
                         reads=[xb, stb, b_g], writes=[obb])
                    k.dma(SP, final_out[s * 128:(s + 1) * 128, :], ot[:], obb, "s", reads=[obb])
                    continue
                ht, hbb = hb[s % 2]
                k.op(DVE, lambda: nc.vector.scalar_tensor_tensor(out=ht[:], in0=xt[:], scalar=stt[:, 1:2],
                                                                 in1=gB[:], op0=ALU.mult, op1=ALU.mult),
                     reads=[xb, stb, b_g], writes=[hbb])
                for q in range(4):
                    hs, hsb = hTs[ti % 2][q]
                    pt, pb = psT[cnt % 4]
                    for j in range(8):
                        kk = q * 8 + j
                        k.op(PE, lambda: nc.tensor.transpose(out=pt[:, j * 128:(j + 1) * 128],
                                                             in_=ht[:, kk * 128:(kk + 1) * 128],
                                                             identity=identb[:]),
                             reads=[hbb, b_id], writes=[pb], inc=(j == 7))
                    srcv = pt[:, :].rearrange("p (j t) -> p j t", j=8)
                    dstv = hs[:, :, so:so + 128]
                    if cnt % 2 == 0:
                        k.op(ACT, lambda: nc.scalar.copy(out=dstv, in_=srcv), reads=[pb], writes=[hsb])
                    else:
                        k.op(DVE, lambda: nc.vector.tensor_copy(out=dstv, in_=srcv), reads=[pb], writes=[hsb])
                    cnt += 1
                if (s % spt) == spt - 1:
                    for q in range(4):
                        hs, hsb = hTs[ti % 2][q]
                        k.dma(SP, dst_scr[ti].rearrange("p (k t) -> p k t", k=KC)[:, q * 8:(q + 1) * 8, :], hs[:], hsb, "s",
                              reads=[hsb], writes=[dst_bufs[ti]])
            k.end_phase()

    def rotary_epi(P, pb, cs, csb, sn, snb, tmp, ps2, outbf, outb, kr_out=None, after=None):
        tmp["i"] = tmp.get("i", 0) + 1
        kb_t, kb_b = tmp["kb"][tmp["i"] % 2]
        t1_t, t1_b = tmp["t1"]
        t2_t, t2_b = tmp["t2"]
        p2, p2b = ps2
        k.op(ACT, lambda: nc.scalar.copy(out=kb_t[:], in_=P), reads=[pb], writes=[kb_b])

        def finish():
            k.op(PE, lambda: nc.tensor.matmul(p2[:, 0:T], permb[:], kb_t[:], start=True, stop=True),
                 reads=[kb_b, b_pm], writes=[p2b])
            k.op(DVE, lambda: nc.vector.tensor_tensor(out=t1_t[:], in0=P, in1=cs, op=ALU.mult),
                 reads=[pb, csb], writes=[t1_b])
            k.op(DVE, lambda: nc.vector.tensor_tensor(out=t2_t[:], in0=p2[:, 0:T], in1=sn, op=ALU.mult),
                 reads=[p2b, snb], writes=[t2_b])
            if kr_out is None:
                k.op(DVE, lambda: nc.vector.tensor_tensor(out=outbf, in0=t1_t[:], in1=t2_t[:], op=ALU.add),
                     reads=[t1_b, t2_b], writes=[outb])
            else:
                kr_t, kr_b = tmp["kr"]
                k.op(DVE, lambda: nc.vector.tensor_tensor(out=kr_t[:], in0=t1_t[:], in1=t2_t[:], op=ALU.add),
                     reads=[t1_b, t2_b], writes=[kr_b])
                k.op(DVE, lambda: nc.vector.reduce_sum(out=kr_out, in_=kr_t[:, :].rearrange("p (b t) -> p b t", b=2),
                                                       axis=AX.X),
                     reads=[kr_b], writes=[b_kmf])
                k.op(ACT, lambda: nc.scalar.copy(out=outbf, in_=kr_t[:]), reads=[kr_b], writes=[outb])
            if after is not None:
                after()
        return finish

    def gelu_tanh(P, pb, shape_fn, tmp, outap, outb):
        a_t, a_b = tmp["ga"]
        b_t, b_b = tmp["gb"]
        av = shape_fn(a_t)
        bv = shape_fn(b_t)
        k.op(ACT, lambda: nc.scalar.activation(out=av, in_=P, func=AF.Square), reads=[pb], writes=[a_b])
        k.op(DVE, lambda: nc.vector.tensor_scalar(out=av, in0=av, scalar1=0.044715, scalar2=1.0,
                                                  op0=ALU.mult, op1=ALU.add), reads=[a_b], writes=[a_b])
        k.op(DVE, lambda: nc.vector.tensor_tensor(out=bv, in0=av, in1=P, op=ALU.mult),
             reads=[a_b, pb], writes=[b_b])
        k.op(ACT, lambda: nc.scalar.activation(out=av, in_=bv, func=AF.Sigmoid, scale=1.5957691216057308),
             reads=[b_b], writes=[a_b])
        k.op(DVE, lambda: nc.vector.tensor_tensor(out=outap, in0=av, in1=P, op=ALU.mult),
             reads=[a_b, pb], writes=[outb])

    def gemm_acc(P, pb, nk, lhs_fn, rhs_fn, rd):
        for kk in range(nk):
            k.op(PE, lambda: nc.tensor.matmul(P, lhs_fn(kk), rhs_fn(kk), start=(kk == 0), stop=(kk == nk - 1)),
                 reads=rd, writes=[pb], inc=(kk == nk - 1))

    if "N1" not in skip:
        norm_phase(x_all, SEQ, g_mix, hT_all, b_hT_all)
        norm_phase(x_own, OWN, g_mix, hT_own, b_hT_own)

    if stop_after == "N1":
        k.barrier()
        k.top.close()
        return nc
    for st in phase("A"):
        ws = WStream(k, st, 6, KC * 256, live=4)
        hTb = [k.sb(st, [128, KC, T], BF16, "hT") for _ in range(2)]
        cst = [k.sb(st, [128, T], F32, "cs") for _ in range(2)]
        snt = [k.sb(st, [128, T], F32, "sn") for _ in range(2)]
        tmp = {"kb": [k.sb(st, [128, T], BF16), k.sb(st, [128, T], BF16)], "t1": k.sb(st, [128, T], F32), "t2": k.sb(st, [128, T], F32),
               "kr": k.sb(st, [128, T], F32)}
        kst = [k.sb(st, [128, T], BF16, "kst") for _ in range(3)]
        vst = [k.sb(st, [128, 8, 4, 128], BF16, "vst") for _ in range(2)]
        pss = [k.ps(st, [128, 512], F32, "ps") for _ in range(6)]
        ps2 = k.ps(st, [128, 512], F32, "ps2")
        reqs = []
        for g in range(8):
            reqs.append((w_in, 0, KC, C_K + g * 256, 256))
        for g in range(8):
            reqs.append((w_in, 0, KC, C_V + g * 256, 256))
        ws.plan(reqs)
        pi = 0
        ki = 0
        pend = None
        li = [0]

        def loadA(tt, with_tables):
            ht, hbf = hTb[li[0] % 2]
            k.dma(SP, ht[:], hT_all[tt].rearrange("p (k t) -> p k t", k=KC), hbf, "l",
                  reads=[b_hT_all[tt]], writes=[hbf])
            if with_tables:
                ct, cb = cst[li[0] % 2]
                stn, sb_ = snt[li[0] % 2]
                k.dma(SP, ct[:], cos_all[:, tt * T:(tt + 1) * T], cb, "l", writes=[cb])
                k.dma(SP, stn[:], sin_all[:, tt * T:(tt + 1) * T], sb_, "l", writes=[sb_])
            li[0] += 1

        loadA(0, True)
        ui = 0
        for sg in range(4):
            Wg = [ws.next() for _ in range(4)]
            for tt in range(NT_ALL):
                ht, hbf = hTb[ui % 2]
                ct, cb = cst[ui % 2]
                stn, sb_ = snt[ui % 2]
                ui += 1
                if pend is not None:
                    pend()
                    pend = None
                if tt + 1 < NT_ALL:
                    loadA(tt + 1, sg < 2)
                elif sg + 1 < 4:
                    loadA(0, sg + 1 < 2)
                if sg < 2:
                    for gl in range(4):
                        W, wb = Wg[gl]
                        for hh in range(2):
                            h = sg * 8 + gl * 2 + hh
                            P, pb = pss[pi % 6]
                            pi += 1
                            gemm_acc(P[:, 0:T], pb, KC, lambda kk: W[:, kk, hh * 128:(hh + 1) * 128],
                                     lambda kk: ht[:, kk, :], [wb, hbf])
                            ko, kob = kst[ki % 3]
                            ki += 1
                            if pend is not None:
                                pend()

                            def store_k(h=h, tt=tt, ko=ko, kob=kob):
                                k.dma(SP, KT_scr[h][:, tt * T:(tt + 1) * T], ko[:], kob, "s", reads=[kob], writes=[b_KT])
                            pend = rotary_epi(P[:, 0:T], pb, ct[:], cb, stn[:], sb_, tmp, ps2, ko[:], kob,
                                              kr_out=kmT_f[:, h, tt * 2:(tt + 1) * 2], after=store_k)
                    if pend is not None and tt == NT_ALL - 1:
                        pend()
                        pend = None
                else:
                    for gl in range(4):
                        W, wb = Wg[gl]
                        for m in range(4):
                            P, pb = pss[pi % 6]
                            pi += 1
                            gemm_acc(P[:, 0:256], pb, KC, lambda kk: ht[:, kk, m * 128:(m + 1) * 128],
                                     lambda kk: W[:, kk, :], [wb, hbf])
                            vt, vb = vst[tt % 2]
                            dsto = vt[:, gl * 2:gl * 2 + 2, m, :]
                            srco = P[:, 0:256].rearrange("p (h d) -> p h d", h=2)
                            if (gl + m) % 2 == 0:
                                k.op(ACT, lambda: nc.scalar.copy(out=dsto, in_=srco), reads=[pb], writes=[vb])
                            else:
                                k.op(DVE, lambda: nc.vector.tensor_copy(out=dsto, in_=srco), reads=[pb], writes=[vb])
                    h0 = (sg - 2) * 8
                    vt, vb = vst[tt % 2]
                    for hq in range(2):
                        dstv = V_scr[h0 + hq * 4:h0 + hq * 4 + 4].rearrange("h p cd -> p h cd")[:, :, tt * 512:(tt + 1) * 512]
                        srcv = vt[:, hq * 4:hq * 4 + 4, :, :].rearrange("p h m d -> p h (m d)")
                        k.dma(SP, dstv, srcv, vb, "s", reads=[vb], writes=[b_V])
        k.op(DVE, lambda: nc.vector.tensor_scalar(out=kmT_b[:], in0=kmT_f[:], scalar1=1.0 / BLK, scalar2=None,
                                                  op0=ALU.mult), reads=[b_kmf], writes=[b_kmb])
        k.end_phase()

    if stop_after == "A":
        k.barrier()
        k.top.close()
        return nc
    for st in phase("B2"):
        ws = WStream(k, st, 3, KC * 256)
        hTb = [k.sb(st, [128, KC, T], BF16, "hT") for _ in range(2)]
        cst = [k.sb(st, [128, T], F32, "cs") for _ in range(2)]
        snt = [k.sb(st, [128, T], F32, "sn") for _ in range(2)]
        tmp = {"kb": [k.sb(st, [128, T], BF16), k.sb(st, [128, T], BF16)], "t1": k.sb(st, [128, T], F32), "t2": k.sb(st, [128, T], F32)}
        qst = [k.sb(st, [128, HEADS, T], BF16, "qst") for _ in range(2)]
        pss = [k.ps(st, [128, 512], F32, "ps") for _ in range(6)]
        ps2 = k.ps(st, [128, 512], F32, "ps2")
        ws.plan([(w_in, 0, KC, C_Q + g * 256, 256) for tt in range(NT_OWN) for g in range(8)])
        pi = 0
        pend = None
        def loadB(tt):
            ht, hbf = hTb[tt % 2]
            k.dma(SP, ht[:], hT_own[tt].rearrange("p (k t) -> p k t", k=KC), hbf, "l",
                  reads=[b_hT_own[tt]], writes=[hbf])
            ct, cb = cst[tt % 2]
            stn, sb_ = snt[tt % 2]
            k.dma(SP, ct[:], cos_own[:, tt * T:(tt + 1) * T], cb, "l", writes=[cb])
            k.dma(SP, stn[:], sin_own[:, tt * T:(tt + 1) * T], sb_, "l", writes=[sb_])

        loadB(0)
        for tt in range(NT_OWN):
            ht, hbf = hTb[tt % 2]
            ct, cb = cst[tt % 2]
            stn, sb_ = snt[tt % 2]
            if tt + 1 < NT_OWN:
                loadB(tt + 1)
            qt, qb = qst[tt % 2]
            for g in range(8):
                W, wb = ws.next()
                for hh in range(2):
                    h = g * 2 + hh
                    P, pb = pss[pi % 6]
                    pi += 1
                    gemm_acc(P[:, 0:T], pb, KC, lambda kk: W[:, kk, hh * 128:(hh + 1) * 128],
                             lambda kk: ht[:, kk, :], [wb, hbf])
                    if pend is not None:
                        pend()
                    pend = rotary_epi(P[:, 0:T], pb, ct[:], cb, stn[:], sb_, tmp, ps2, qt[:, h, :], qb)
            pend()
            pend = None
            k.dma(SP, QT_scr[tt].rearrange("p (h t) -> p h t", h=HEADS), qt[:], qb, "s", reads=[qb],
                  writes=[b_QT[tt]])
        k.end_phase()

    if stop_after == "B2":
        k.barrier()
        k.top.close()
        return nc
    for st in phase("B3"):
        ws = WStream(k, st, 3, KC * 256)
        hTb = [k.sb(st, [128, KC, T], BF16, "hT") for _ in range(1)]
        guT, b_gu = k.sb(st, [128, 16, T], BF16, "guT")
        vsb = [k.sb(st, [128, 2048], F32, "vsb") for _ in range(4)]
        vln = [k.sb(st, [128, 2048], BF16, "vln") for _ in range(2)]
        lngB, b_lng = k.sb(st, [128, 2048], F32, "lng")
        lnbB, b_lnb = k.sb(st, [128, 2048], F32, "lnb")
        bsB, b_bs = k.sb(st, [128, 16, 128], F32, "bsB")
        wTm, b_wT = k.sb(st, [128, 16, 128], BF16, "wTm")
        trif, b_tri = k.sb(st, [128, 128], F32, "tri")
        obst = [k.sb(st, [128, 16, T], BF16, "obst") for _ in range(1)]
        tmpA = [{"ga": k.sb(st, [128, T], F32), "gb": k.sb(st, [128, T], F32)} for _ in range(2)]
        lnt, b_lnt = k.sb(st, [128, 2048], F32, "lnt")
        wTf, b_wTf = lnt[:, :].rearrange("p (g t) -> p g t", g=16), b_lnt
        junk, b_junk = k.sb(st, [128, 2048], BF16, "junk")
        stat = [k.sb(st, [128, 8], F32, "stat") for _ in range(2)]
        mt, b_mt = k.sb(st, [128, 4, 128], F32, "mt")
        pss = [k.ps(st, [128, 512], F32, "ps") for _ in range(8)]
        k.dma(SP, lngB[:], pbc(ln_g), b_lng, "l", writes=[b_lng])
        k.dma(SP, lnbB[:], pbc(ln_b), b_lnb, "l", writes=[b_lnb])
        k.dma(SP, bsB[:], pbc(b_s).rearrange("p (g t) -> p g t", g=16), b_bs, "l",
              writes=[b_bs])
        k.dma(SP, wTf, w_sT.rearrange("p (g t) -> p g t", g=16), b_wTf, "l", writes=[b_wTf])
        k.dma(SP, trif[:], tri, b_tri, "l", writes=[b_tri])
        for g in range(16):
            k.op(DVE, lambda: nc.vector.tensor_tensor(out=wTm[:, g, :], in0=wTf[:, g, :], in1=trif[:], op=ALU.mult),
                 reads=[b_wTf, b_tri], writes=[b_wT])
        reqs = []
        for tt in range(NT_OWN):
            for g in range(8):
                reqs.append((w_in, 0, KC, C_U + g * 256, 256))
            for g in range(8):
                reqs.append((w_in, 0, KC, C_VS + g * 256, 256))
        ws.plan(reqs)
        pi = 0
        gi = 0
        ht, hbf = hTb[0]

        def loadB3(tt):
            k.dma(SP, ht[:], hT_own[tt].rearrange("p (k t) -> p k t", k=KC), hbf, "l",
                  reads=[b_hT_own[tt]], writes=[hbf])

        loadB3(0)
        for tt in range(NT_OWN):
            for g in range(8):
                W, wb = ws.next()
                for hh in range(2):
                    gg = g * 2 + hh
                    P, pb = pss[pi % 8]
                    pi += 1
                    gemm_acc(P[:, 0:T], pb, KC, lambda kk: W[:, kk, hh * 128:(hh + 1) * 128],
                             lambda kk: ht[:, kk, :], [wb, hbf])
                    gelu_tanh(P[:, 0:T], pb, lambda t: t[:, 0:T], tmpA[gi % 2], guT[:, gg, :], b_gu)
                    gi += 1
            for g in range(8):
                W, wb = ws.next()
                for m in range(4):
                    P, pb = pss[pi % 8]
                    pi += 1
                    gemm_acc(P[:, 0:256], pb, KC, lambda kk: ht[:, kk, m * 128:(m + 1) * 128],
                             lambda kk: W[:, kk, :], [wb, hbf])
                    vt, vb = vsb[m]
                    gelu_tanh(P[:, 0:256], pb, lambda t: t[:, 0:256], tmpA[gi % 2], vt[:, g * 256:(g + 1) * 256], vb)
                    gi += 1
            if tt + 1 < NT_OWN:
                loadB3(tt + 1)
            ot, obb = obst[0]
            for m in range(4):
                vt, vb = vsb[m]
                stt, stb = stat[m % 2]
                k.op(DVE, lambda: nc.vector.reduce_sum(out=stt[:, 0:1], in_=vt[:], axis=AX.X), reads=[vb], writes=[stb])
                k.op(ACT, lambda: nc.scalar.activation(out=junk[:], in_=vt[:], func=AF.Square, accum_out=stt[:, 1:2]),
                     reads=[vb], writes=[b_junk, stb])
                k.op(DVE, lambda: nc.vector.tensor_scalar(out=stt[:, 2:4], in0=stt[:, 0:2], scalar1=1.0 / 2048,
                                                          scalar2=None, op0=ALU.mult), reads=[stb], writes=[stb])
                k.op(DVE, lambda: nc.vector.tensor_tensor(out=stt[:, 4:5], in0=stt[:, 2:3], in1=stt[:, 2:3], op=ALU.mult),
                     reads=[stb], writes=[stb])
                k.op(DVE, lambda: nc.vector.tensor_tensor(out=stt[:, 5:6], in0=stt[:, 3:4], in1=stt[:, 4:5], op=ALU.subtract),
                     reads=[stb], writes=[stb])
                k.op(DVE, lambda: nc.vector.tensor_scalar(out=stt[:, 6:7], in0=stt[:, 5:6], scalar1=LN_EPS, scalar2=None,
                                                          op0=ALU.add), reads=[stb], writes=[stb])
                k.op(ACT, lambda: nc.scalar.activation(out=stt[:, 6:7], in_=stt[:, 6:7], func=AF.Sqrt),
                     reads=[stb], writes=[stb])
                k.op(DVE, lambda: nc.vector.reciprocal(out=stt[:, 6:7], in_=stt[:, 6:7]),
                     reads=[stb], writes=[stb])
                k.op(DVE, lambda: nc.vector.tensor_scalar(out=lnt[:], in0=vt[:], scalar1=stt[:, 2:3], scalar2=stt[:, 6:7],
                                                          op0=ALU.subtract, op1=ALU.mult), reads=[vb, stb], writes=[b_lnt])
                k.op(DVE, lambda: nc.vector.tensor_tensor(out=lnt[:], in0=lnt[:], in1=lngB[:], op=ALU.mult),
                     reads=[b_lnt, b_lng], writes=[b_lnt])
                vl, vlb = vln[m % 2]
                k.op(DVE, lambda: nc.vector.tensor_tensor(out=vl[:], in0=lnt[:], in1=lnbB[:], op=ALU.add),
                     reads=[b_lnt, b_lnb], writes=[vlb])
                for gq in range(4):
                    P, pb = pss[pi % 8]
                    pi += 1
                    for gg in range(4):
                        g = gq * 4 + gg
                        k.op(PE, lambda: nc.tensor.matmul(P[:, gg * 128:(gg + 1) * 128], vl[:, g * 128:(g + 1) * 128],
                                                          wTm[:, g, :], start=True, stop=True),
                             reads=[vlb, b_wT], writes=[pb], inc=(gg == 3))
                    k.op(DVE, lambda: nc.vector.tensor_tensor(out=mt[:], in0=P[:, :].rearrange("p (g t) -> p g t", g=4),
                                                              in1=bsB[:, gq * 4:(gq + 1) * 4, :], op=ALU.add),
                         reads=[pb, b_bs], writes=[b_mt])
                    k.op(DVE, lambda: nc.vector.tensor_tensor(out=ot[:, gq * 4:(gq + 1) * 4, m * 128:(m + 1) * 128],
                                                              in0=mt[:], in1=guT[:, gq * 4:(gq + 1) * 4, m * 128:(m + 1) * 128],
                                                              op=ALU.mult),
                         reads=[b_mt, b_gu], writes=[obb])
            k.dma(SP, obT_scr[tt].rearrange("p (g t) -> p g t", g=16), ot[:], obb, "s", reads=[obb],
                  writes=[b_obT[tt]])
        k.end_phase()

    if stop_after == "B3":
        k.barrier()
        k.top.close()
        return nc
    for st in phase("C"):
        qtb = [k.sb(st, [128, HEADS, T], BF16, "qt") for _ in range(2)]
        kbuf = [k.sb(st, [128, SEQ], BF16, "kbuf") for _ in range(2)]
        vbuf = [k.sb(st, [128, 64, 132], BF16, "vbuf") for _ in range(2)]
        negB, b_neg = k.sb(st, [128, 8, 32], F32, "negB")
        valB, b_val = k.sb(st, [128, 8, 32], F32, "valB")
        ownB, b_own = k.sb(st, [128, 8, 32], F32, "ownB")
        dmk, b_dmk = k.sb(st, [128, 4, 512], BF16, "dmk")
        gms = [k.sb(st, [128, 2, 32], F32, "gms") for _ in range(4)]
        m8 = [k.sb(st, [128, 16], F32, "m8") for _ in range(4)]
        selA, b_selA = k.sb(st, [128, HEADS * 2, 2, 32], F32, "selA")
        ptb = [k.sb(st, [128, 512], BF16, "pt") for _ in range(4)]
        acc = [k.sb(st, [128, 2, 132], F32, "acc") for _ in range(2)]
        rden = [k.sb(st, [128, 2], F32, "rden") for _ in range(2)]
        oat, b_oat = k.sb(st, [128, 4, 2048], BF16, "oat")
        oast = [k.sb(st, [128, 16, T], BF16, "oast") for _ in range(1)]
        psS = [k.ps(st, [128, 512], F32, "psS") for _ in range(3)]
        psO = [k.ps(st, [128, 512], F32, "psO") for _ in range(2)]
        psG = [k.ps(st, [128, 512], F32, "psG") for _ in range(1)]
        psT = [k.ps(st, [128, 1024], BF16, "psT") for _ in range(2)]
        k.dma(SP, negB[:], pbc(negb).rearrange("p (i n) -> p i n", i=8), b_neg, "l", writes=[b_neg])
        k.dma(SP, valB[:], pbc(validm).rearrange("p (i n) -> p i n", i=8), b_val, "l", writes=[b_val])
        k.dma(SP, ownB[:], pbc(ownm).rearrange("p (i n) -> p i n", i=8), b_own, "l", writes=[b_own])
        k.dma(POOL, dmk[:], dmask.rearrange("p (c q) -> p c q", c=4), b_dmk, "l", writes=[b_dmk])
        for vt, vb in vbuf:
            k.op(DVE, lambda: nc.vector.memset(vt[:, :, 128:132], 1.0), writes=[vb])
        si = 0
        oi = 0
        gi = 0
        pti = 0
        for tt in range(NT_OWN):
            qt, qb = qtb[tt % 2]
            if tt == 0:
                k.dma(SP, qt[:], QT_scr[0].rearrange("p (h t) -> p h t", h=HEADS), qb, "l", reads=[b_QT[0]], writes=[qb])
            if tt + 1 < NT_OWN:
                qn, qnb = qtb[(tt + 1) % 2]
                k.dma(SP, qn[:], QT_scr[tt + 1].rearrange("p (h t) -> p h t", h=HEADS), qnb, "l", reads=[b_QT[tt + 1]],
                      writes=[qnb])
            nkb = 8 * tt + 8
            for h in range(HEADS):
                Pg, pgb = psG[0]
                for li in range(2):
                    q0 = li * 256
                    for s in range(2):
                        k.op(PE, lambda: nc.tensor.matmul(Pg[:, (li * 2 + s) * 32:(li * 2 + s + 1) * 32],
                                                          qt[:, h, q0 + s * 128:q0 + (s + 1) * 128],
                                                          kmT_b[:, h, :], start=True, stop=True),
                             reads=[qb, b_kmb], writes=[pgb], inc=(li == 1 and s == 1))
                for li in range(2):
                    i = 2 * tt + li
                    gt, gb_ = gms[gi % 4]
                    m8t, m8b = m8[gi % 4]
                    gi += 1
                    k.op(DVE, lambda: nc.vector.tensor_tensor(
                        out=gt[:], in0=Pg[:, li * 64:(li + 1) * 64].rearrange("p (s n) -> p s n", s=2),
                        in1=negB[:, i:i + 1, :].to_broadcast([128, 2, 32]), op=ALU.add),
                         reads=[pgb, b_neg], writes=[gb_])
                    for s in range(2):
                        k.op(DVE, lambda: nc.vector.max(out=m8t[:, s * 8:(s + 1) * 8], in_=gt[:, s, :]),
                             reads=[gb_], writes=[m8b])
                    for s in range(2):
                        k.op(DVE, lambda: nc.vector.scalar_tensor_tensor(out=selA[:, h * 2 + li, s, :], in0=gt[:, s, :],
                                                                         scalar=m8t[:, s * 8 + 2:s * 8 + 3],
                                                                         in1=valB[:, i, :], op0=ALU.is_ge, op1=ALU.mult),
                             reads=[gb_, m8b, b_val], writes=[b_selA])
            for li in range(2):
                i = 2 * tt + li
                k.op(DVE, lambda: nc.vector.tensor_tensor(
                    out=selA[:, li::2, :, :], in0=selA[:, li::2, :, :],
                    in1=ownB[:, i:i + 1, :].unsqueeze(1).to_broadcast([128, HEADS, 2, 32]), op=ALU.add),
                     reads=[b_selA, b_own], writes=[b_selA])
            for h in range(HEADS):
                kt, kbb = kbuf[h % 2]
                vt, vb = vbuf[h % 2]
                k.dma(SP, kt[:, 0:nkb * BLK], KT_scr[h][:, 0:nkb * BLK], kbb, "l", reads=[b_KT], writes=[kbb])
                half = nkb
                vsrc = V_scr[h].rearrange("p (c d) -> p c d", c=64)
                k.dma(SP, vt[:, 0:half, 0:128], vsrc[:, 0:half, :], vb, "l", reads=[b_V], writes=[vb])
                k.dma(SP, vt[:, half:2 * half, 0:128], vsrc[:, half:2 * half, :], vb, "l", reads=[b_V], writes=[vb])
                for li in range(2):
                    i = 2 * tt + li
                    q0 = li * 256
                    at, ab = acc[li]
                    nblk = 4 * i + 4

                    def qk(n):
                        nonlocal si
                        Ps, psb = psS[si % 3]
                        si += 1
                        for c2 in range(2):
                            k.op(PE, lambda: nc.tensor.matmul(Ps[:, c2 * 256:(c2 + 1) * 256],
                                                              kt[:, n * 256 + c2 * 128:n * 256 + (c2 + 1) * 128],
                                                              qt[:, h, q0:q0 + 256], start=True, stop=True),
                                 reads=[kbb, qb], writes=[psb], inc=(c2 == 1))
                        return Ps, psb

                    cur = qk(0)
                    for n in range(nblk):
                        Ps, psb = cur
                        if n + 1 < nblk:
                            cur = qk(n + 1)
                        pt, ptbb = ptb[pti % 4]
                        pti += 1
                        k.op(ACT, lambda: nc.scalar.activation(out=pt[:], in_=Ps[:, :], func=AF.Exp, scale=SCALE),
                             reads=[psb], writes=[ptbb])
                        if n >= 4 * i:
                            c = n - 4 * i
                            k.op(DVE, lambda: nc.vector.tensor_tensor(out=pt[:], in0=pt[:], in1=dmk[:, c, :], op=ALU.mult),
                                 reads=[ptbb, b_dmk], writes=[ptbb])
                        Po, pob = psO[oi % 2]
                        oi += 1
                        for s in range(2):
                            for c2 in range(2):
                                k.op(PE, lambda: nc.tensor.matmul(Po[:, s * 132:s * 132 + 129],
                                                                  pt[:, c2 * 256 + s * 128:c2 * 256 + (s + 1) * 128],
                                                                  vt[:, n * 2 + c2, 0:129], start=(c2 == 0), stop=(c2 == 1)),
                                     reads=[ptbb, vb], writes=[pob], inc=(s == 1 and c2 == 1))
                        for s in range(2):
                            selv = selA[:, h * 2 + li, s, n:n + 1]
                            if n == 0:
                                k.op(DVE, lambda: nc.vector.tensor_scalar(out=at[:, s, 0:129], in0=Po[:, s * 132:s * 132 + 129],
                                                                          scalar1=selv, scalar2=None, op0=ALU.mult),
                                     reads=[pob, b_selA], writes=[ab])
                            else:
                                k.op(DVE, lambda: nc.vector.scalar_tensor_tensor(out=at[:, s, 0:129],
                                                                                 in0=Po[:, s * 132:s * 132 + 129],
                                                                                 scalar=selv, in1=at[:, s, 0:129],
                                                                                 op0=ALU.mult, op1=ALU.add),
                                     reads=[pob, b_selA, ab], writes=[ab])
                    rt, rb = rden[li]
                    k.op(DVE, lambda: nc.vector.reciprocal(out=rt[:, 0:2], in_=at[:, :, 128]), reads=[ab], writes=[rb])
                    for s in range(2):
                        k.op(DVE, lambda: nc.vector.tensor_scalar(out=oat[:, li * 2 + s, h * 128:(h + 1) * 128],
                                                                  in0=at[:, s, 0:128], scalar1=rt[:, s:s + 1],
                                                                  scalar2=None, op0=ALU.mult),
                             reads=[ab, rb], writes=[b_oat])
            ost, osb = oast[0]
            tci = 0
            for sub in range(4):
                for hq in range(2):
                    pT, pTb = psT[tci % 2]
                    for j in range(8):
                        hh = hq * 8 + j
                        k.op(PE, lambda: nc.tensor.transpose(out=pT[:, j * 128:(j + 1) * 128],
                                                             in_=oat[:, sub, hh * 128:(hh + 1) * 128], identity=identb[:]),
                             reads=[b_oat, b_id], writes=[pTb], inc=(j == 7))
                    srcv = pT[:, :].rearrange("p (j t) -> p j t", j=8)
                    dstv = ost[:, hq * 8:(hq + 1) * 8, sub * 128:(sub + 1) * 128]
                    if tci % 2 == 0:
                        k.op(ACT, lambda: nc.scalar.copy(out=dstv, in_=srcv), reads=[pTb], writes=[osb])
                    else:
                        k.op(DVE, lambda: nc.vector.tensor_copy(out=dstv, in_=srcv), reads=[pTb], writes=[osb])
                    tci += 1
            k.dma(SP, oaT_scr[tt].rearrange("p (h t) -> p h t", h=16), ost[:], osb, "s", reads=[osb],
                  writes=[b_oaT[tt]])
        k.end_phase()

    if stop_after == "C":
        k.barrier()
        k.top.close()
        return nc
    def resid_gemm(st_outer, ws, inT, inTb, nk, w_ap, ncg, ncols, x_src, x_src_bufs, x_dst, x_dst_bufs, tt, pss, pi,
                   xps, xos):
        for ng in range(ncg):
            W, wb = ws.next()
            xp, xpb = xps[ng % len(xps)]
            xo, xob = xos[ng % len(xos)]
            srcv = x_src[tt * T:(tt + 1) * T, ng * ncols:(ng + 1) * ncols].rearrange("(m p) c -> p m c", p=128)
            k.dma(SP, xp[:, :, 0:ncols], srcv, xpb, "l", reads=[x_src_bufs[tt]] if x_src_bufs else [], writes=[xpb])
            for m in range(4):
                P, pb = pss[pi[0] % len(pss)]
                pi[0] += 1
                gemm_acc(P[:, 0:ncols], pb, nk, lambda kk: inT[:, kk, m * 128:(m + 1) * 128], lambda kk: W[:, kk, :],
                         [wb, inTb])
                k.op(DVE, lambda: nc.vector.tensor_tensor(out=xo[:, m, 0:ncols], in0=P[:, 0:ncols], in1=xp[:, m, 0:ncols],
                                                          op=ALU.add),
                     reads=[pb, xpb], writes=[xob])
            dstv = x_dst[tt * T:(tt + 1) * T, ng * ncols:(ng + 1) * ncols].rearrange("(m p) c -> p m c", p=128)
            k.dma(SP, dstv, xo[:, :, 0:ncols], xob, "s", reads=[xob], writes=[x_dst_bufs[tt]])

    for st in phase("D"):
        ws = WStream(k, st, 3, KC * 256, live=2)
        wsP = WStream(k, st, 3, 16 * 256, live=2)
        hTb = k.sb(st, [128, KC, T], BF16, "hT")
        oaTb = k.sb(st, [128, 16, T], BF16, "oaT")
        obTb = k.sb(st, [128, 16, T], BF16, "obT")
        mT, b_mT = k.sb(st, [128, KC, T], BF16, "mT")
        sa = [k.sb(st, [128, T], F32, "sa") for _ in range(2)]
        sbb = [k.sb(st, [128, T], F32, "sb") for _ in range(2)]
        t1 = [k.sb(st, [128, T], F32, "t1") for _ in range(2)]
        t2 = [k.sb(st, [128, T], F32, "t2") for _ in range(2)]
        xps = [k.sb(st, [128, 4, 256], F32, "xp") for _ in range(2)]
        xos = [k.sb(st, [128, 4, 256], F32, "xo") for _ in range(2)]
        pss = [k.ps(st, [128, 512], F32, "ps") for _ in range(8)]
        reqs = []
        reqsP = []
        for tt in range(NT_OWN):
            for cg in range(16):
                reqs.append((w_in, 0, KC, C_GA + cg * 256, 256))
                reqs.append((w_in, 0, KC, C_GB + cg * 256, 256))
                reqsP.append((w_pa, 0, 16, cg * 256, 256))
                reqsP.append((w_pb, 0, 16, cg * 256, 256))
            for ng in range(16):
                reqs.append((w_out, 0, KC, ng * 256, 256))
        ws.plan(reqs)
        wsP.plan(reqsP)
        pi = [0]
        ei = 0
        ht, hbf = hTb
        oa, oab = oaTb
        ob_, obb = obTb

        def loadD(tt):
            k.dma(SP, ht[:], hT_own[tt].rearrange("p (k t) -> p k t", k=KC), hbf, "l", reads=[b_hT_own[tt]], writes=[hbf])
            k.dma(SP, oa[:], oaT_scr[tt].rearrange("p (k t) -> p k t", k=16), oab, "l", reads=[b_oaT[tt]], writes=[oab])
            k.dma(SP, ob_[:], obT_scr[tt].rearrange("p (k t) -> p k t", k=16), obb, "l", reads=[b_obT[tt]], writes=[obb])

        loadD(0)
        for tt in range(NT_OWN):
            for cg in range(16):
                Wga, wgab = ws.next()
                Wgb, wgbb = ws.next()
                WA, wab = wsP.next()
                WB, wbb = wsP.next()
                for cc in range(2):
                    c = cg * 2 + cc
                    Pga, pgab = pss[pi[0] % 8]
                    Pgb, pgbb = pss[(pi[0] + 1) % 8]
                    PA, pab = pss[(pi[0] + 2) % 8]
                    PB, pbb = pss[(pi[0] + 3) % 8]
                    pi[0] += 4
                    cs_ = slice(cc * 128, (cc + 1) * 128)
                    gemm_acc(Pga[:, 0:T], pgab, KC, lambda kk: Wga[:, kk, cs_], lambda kk: ht[:, kk, :], [wgab, hbf])
                    gemm_acc(Pgb[:, 0:T], pgbb, KC, lambda kk: Wgb[:, kk, cs_], lambda kk: ht[:, kk, :], [wgbb, hbf])
                    gemm_acc(PA[:, 0:T], pab, 16, lambda kk: WA[:, kk, cs_], lambda kk: oa[:, kk, :], [wab, oab])
                    gemm_acc(PB[:, 0:T], pbb, 16, lambda kk: WB[:, kk, cs_], lambda kk: ob_[:, kk, :], [wbb, obb])
                    sat, sab = sa[ei % 2]
                    sbt, sbbb = sbb[ei % 2]
                    t1t, t1b = t1[ei % 2]
                    t2t, t2b = t2[ei % 2]
                    ei += 1
                    k.op(ACT, lambda: nc.scalar.activation(out=sat[:], in_=Pga[:, 0:T], func=AF.Sigmoid),
                         reads=[pgab], writes=[sab])
                    k.op(ACT, lambda: nc.scalar.activation(out=sbt[:], in_=Pgb[:, 0:T], func=AF.Sigmoid),
                         reads=[pgbb], writes=[sbbb])
                    k.op(DVE, lambda: nc.vector.tensor_tensor(out=t1t[:], in0=sat[:], in1=PA[:, 0:T], op=ALU.mult),
                         reads=[sab, pab], writes=[t1b])
                    k.op(DVE, lambda: nc.vector.tensor_tensor(out=t2t[:], in0=sbt[:], in1=PB[:, 0:T], op=ALU.mult),
                         reads=[sbbb, pbb], writes=[t2b])
                    k.op(DVE, lambda: nc.vector.tensor_tensor(out=mT[:, c, :], in0=t1t[:], in1=t2t[:], op=ALU.add),
                         reads=[t1b, t2b], writes=[b_mT])
            if tt + 1 < NT_OWN:
                loadD(tt + 1)
            resid_gemm(st, ws, mT, b_mT, KC, w_out, 16, 256, x_own, None, x1_scr, b_x1, tt, pss, pi, xps, xos)
        k.end_phase()

    if stop_after == "D":
        k.barrier()
        k.top.close()
        return nc
    if "N2" not in skip:
        norm_phase(x1_scr, OWN, g_xat, h2T_scr, b_h2T, src_bufs=b_x1)
        norm_phase(mem_b, NMEM, g_mem, memT_scr, b_memT, tile_tok=NMEM)

    if stop_after == "N2":
        k.barrier()
        k.top.close()
        return nc
    for st in phase("E"):
        ws = WStream(k, st, 2, KC * 256)
        memT, b_mT2 = k.sb(st, [128, KC, NMEM], BF16, "memT")
        KxT, b_Kx = k.sb(st, [128, XH, NMEM], BF16, "KxT")
        Vx, b_Vx = k.sb(st, [128, 2, XH, 132], BF16, "Vx")
        hTb = k.sb(st, [128, KC, T], BF16, "hT")
        QxT, b_Qx = k.sb(st, [128, XH, T], BF16, "QxT")
        ptb = [k.sb(st, [128, T], BF16, "pt") for _ in range(4)]
        rden = [k.sb(st, [128, 2], F32, "rden") for _ in range(2)]
        oxt, b_oxt = k.sb(st, [128, 4, 512], BF16, "oxt")
        oxT, b_oxT = k.sb(st, [128, XH, T], BF16, "oxT")
        xps = [k.sb(st, [128, 4, 512], F32, "xp") for _ in range(1)]
        xos = [k.sb(st, [128, 4, 512], F32, "xo") for _ in range(1)]
        pss = [k.ps(st, [128, 512], F32, "ps") for _ in range(6)]
        psT = [k.ps(st, [128, 1024], BF16, "psT") for _ in range(2)]
        ws.plan([(w_xkv, 0, KC, g * 256, 256) for g in range(4)])
        pi = [0]
        k.dma(SP, memT[:], memT_scr[0].rearrange("p (k t) -> p k t", k=KC), b_mT2, "l", reads=[b_memT[0]], writes=[b_mT2])
        k.op(DVE, lambda: nc.vector.memset(Vx[:, :, :, 128:132], 1.0), writes=[b_Vx])
        for g in range(2):
            W, wb = ws.next()
            for hh in range(2):
                h = g * 2 + hh
                P, pb = pss[pi[0] % 6]
                pi[0] += 1
                gemm_acc(P[:, 0:NMEM], pb, KC, lambda kk: W[:, kk, hh * 128:(hh + 1) * 128], lambda kk: memT[:, kk, :],
                         [wb, b_mT2])
                k.op(ACT, lambda: nc.scalar.copy(out=KxT[:, h, :], in_=P[:, 0:NMEM]), reads=[pb], writes=[b_Kx])
        for g in range(2):
            W, wb = ws.next()
            for mc in range(2):
                P, pb = pss[pi[0] % 6]
                pi[0] += 1
                gemm_acc(P[:, 0:256], pb, KC, lambda kk: memT[:, kk, mc * 128:(mc + 1) * 128], lambda kk: W[:, kk, :],
                         [wb, b_mT2])
                k.op(ACT, lambda: nc.scalar.copy(out=Vx[:, mc, 2 * g:2 * g + 2, 0:128],
                                                 in_=P[:, 0:256].rearrange("p (h d) -> p h d", h=2)),
                     reads=[pb], writes=[b_Vx])
        pti = 0
        wsq_s = WStream(k, st, 2, KC * 256, live=2)
        wsq_s.plan([(w_xq, 0, KC, g * 256, 256) for g in range(2)])
        wsq = WRes(wsq_s, 2)
        wso_s = WStream(k, st, 8, 4 * 512, live=8)
        wso_s.plan([(w_xo, 0, 4, ng * 512, 512) for ng in range(8)])
        wso = WRes(wso_s, 8)
        ht, hbf = hTb

        def loadE(tt):
            k.dma(SP, ht[:], h2T_scr[tt].rearrange("p (k t) -> p k t", k=KC), hbf, "l", reads=[b_h2T[tt]], writes=[hbf])

        loadE(0)
        for tt in range(NT_OWN):
            for g in range(2):
                W, wb = wsq.next()
                for hh in range(2):
                    h = g * 2 + hh
                    P, pb = pss[pi[0] % 6]
                    pi[0] += 1
                    gemm_acc(P[:, 0:T], pb, KC, lambda kk: W[:, kk, hh * 128:(hh + 1) * 128], lambda kk: ht[:, kk, :],
                             [wb, hbf])
                    k.op(ACT, lambda: nc.scalar.copy(out=QxT[:, h, :], in_=P[:, 0:T]), reads=[pb], writes=[b_Qx])
            if tt + 1 < NT_OWN:
                loadE(tt + 1)
            for h in range(XH):
                pts = []
                for c2 in range(2):
                    P, pb = pss[pi[0] % 6]
                    pi[0] += 1
                    k.op(PE, lambda: nc.tensor.matmul(P[:, 0:T], KxT[:, h, c2 * 128:(c2 + 1) * 128], QxT[:, h, :],
                                                      start=True, stop=True), reads=[b_Kx, b_Qx], writes=[pb])
                    pt, ptbb = ptb[pti % 4]
                    pti += 1
                    k.op(ACT, lambda: nc.scalar.activation(out=pt[:], in_=P[:, 0:T], func=AF.Exp, scale=SCALE),
                         reads=[pb], writes=[ptbb])
                    pts.append((pt, ptbb))
                for s in range(4):
                    P, pb = pss[pi[0] % 6]
                    pi[0] += 1
                    for c2 in range(2):
                        pt, ptbb = pts[c2]
                        k.op(PE, lambda: nc.tensor.matmul(P[:, 0:129], pt[:, s * 128:(s + 1) * 128], Vx[:, c2, h, 0:129],
                                                          start=(c2 == 0), stop=(c2 == 1)),
                             reads=[ptbb, b_Vx], writes=[pb], inc=(c2 == 1))
                    rt, rb = rden[s % 2]
                    k.op(DVE, lambda: nc.vector.reciprocal(out=rt[:, 0:1], in_=P[:, 128:129]), reads=[pb], writes=[rb])
                    k.op(DVE, lambda: nc.vector.tensor_scalar(out=oxt[:, s, h * 128:(h + 1) * 128], in0=P[:, 0:128],
                                                              scalar1=rt[:, 0:1], scalar2=None, op0=ALU.mult),
                         reads=[pb, rb], writes=[b_oxt])
            for s in range(4):
                pT, pTb = psT[s % 2]
                for j in range(4):
                    k.op(PE, lambda: nc.tensor.transpose(out=pT[:, j * 128:(j + 1) * 128], in_=oxt[:, s, j * 128:(j + 1) * 128],
                                                         identity=identb[:]),
                         reads=[b_oxt, b_id], writes=[pTb], inc=(j == 3))
                k.op(ACT, lambda: nc.scalar.copy(out=oxT[:, :, s * 128:(s + 1) * 128],
                                                 in_=pT[:, 0:512].rearrange("p (j t) -> p j t", j=4)),
                     reads=[pTb], writes=[b_oxT])
            resid_gemm(st, wso, oxT, b_oxT, 4, w_xo, 8, 512, x1_scr, b_x1, x2_scr, b_x2, tt, pss, pi, xps, xos)
        k.end_phase()

    if stop_after == "E":
        k.barrier()
        k.top.close()
        return nc
    if "N3" not in skip:
        norm_phase(x2_scr, OWN, g_ffn, h3T_scr, b_h3T, src_bufs=b_x2)
    for st in phase("F"):
        ws = WStream(k, st, 3, 43 * 256, live=2)
        hTb = k.sb(st, [128, KC, T], BF16, "hT")
        actT, b_act = k.sb(st, [128, FC, T], BF16, "actT")
        sg = [k.sb(st, [128, T], F32, "sg") for _ in range(2)]
        xps = [k.sb(st, [128, 4, 256], F32, "xp") for _ in range(1)]
        xos = [k.sb(st, [128, 4, 256], F32, "xo") for _ in range(1)]
        pss = [k.ps(st, [128, 512], F32, "ps") for _ in range(8)]
        reqs = []
        for tt in range(NT_OWN):
            for fg in range(43):
                reqs.append((w_fg, 0, KC, fg * 256, 256))
                reqs.append((w_fu, 0, KC, fg * 256, 256))
            for ng in range(16):
                reqs.append((w_fd, 0, 43, ng * 256, 256))
                reqs.append((w_fd, 43, 43, ng * 256, 256))
        ws.plan(reqs)
        pi = 0
        ei = 0
        ht, hbf = hTb

        def loadF(tt):
            k.dma(SP, ht[:], h3T_scr[tt].rearrange("p (k t) -> p k t", k=KC), hbf, "l", reads=[b_h3T[tt]], writes=[hbf])

        loadF(0)
        for tt in range(NT_OWN):
            for fg in range(43):
                Wg, wgb = ws.next()
                Wu, wub = ws.next()
                for cc in range(2):
                    c = fg * 2 + cc
                    Pg, pgb = pss[pi % 8]
                    Pu, pub = pss[(pi + 1) % 8]
                    pi += 2
                    cs_ = slice(cc * 128, (cc + 1) * 128)
                    gemm_acc(Pg[:, 0:T], pgb, KC, lambda kk: Wg[:, kk, cs_], lambda kk: ht[:, kk, :], [wgb, hbf])
                    gemm_acc(Pu[:, 0:T], pub, KC, lambda kk: Wu[:, kk, cs_], lambda kk: ht[:, kk, :], [wub, hbf])
                    sgt, sgb = sg[ei % 2]
                    ei += 1
                    k.op(ACT, lambda: nc.scalar.activation(out=sgt[:], in_=Pg[:, 0:T], func=AF.Silu),
                         reads=[pgb], writes=[sgb])
                    k.op(DVE, lambda: nc.vector.tensor_tensor(out=actT[:, c, :], in0=sgt[:], in1=Pu[:, 0:T], op=ALU.mult),
                         reads=[sgb, pub], writes=[b_act])
            if tt + 1 < NT_OWN:
                loadF(tt + 1)
            for ng in range(16):
                xp, xpb = xps[0]
                xo, xob = xos[0]
                srcv = x2_scr[tt * T:(tt + 1) * T, ng * 256:(ng + 1) * 256].rearrange("(m p) c -> p m c", p=128)
                k.dma(SP, xp[:], srcv, xpb, "l", reads=[b_x2[tt]], writes=[xpb])
                Ps = [pss[(pi + m) % 8] for m in range(4)]
                pi += 4
                for half in range(2):
                    W, wb = ws.next()
                    for m in range(4):
                        P, pb = Ps[m]
                        for kk in range(43):
                            k.op(PE, lambda: nc.tensor.matmul(P[:, 0:256], actT[:, half * 43 + kk, m * 128:(m + 1) * 128],
                                                              W[:, kk, :], start=(half == 0 and kk == 0),
                                                              stop=(half == 1 and kk == 42)),
                                 reads=[wb, b_act], writes=[pb], inc=(kk == 42))
                for m in range(4):
                    P, pb = Ps[m]
                    k.op(DVE, lambda: nc.vector.tensor_tensor(out=xo[:, m, :], in0=P[:, 0:256], in1=xp[:, m, :], op=ALU.add),
                         reads=[pb, xpb], writes=[xob])
                dstv = x3_scr[tt * T:(tt + 1) * T, ng * 256:(ng + 1) * 256].rearrange("(m p) c -> p m c", p=128)
                k.dma(SP, dstv, xo[:], xob, "s", reads=[xob], writes=[b_x3[tt]])
        k.end_phase()

    if stop_after == "F":
        k.barrier()
        k.top.close()
        return nc
    if "G" not in skip:
        norm_phase(x3_scr, OWN, g_fin, None, None, src_bufs=b_x3, final_out=out)
    k.top.close()
    return nc


_PROG = None


def _host_consts():
    half = 64
    inv_freq = np.power(np.float32(10000.0), -(np.arange(half, dtype=np.float32) * np.float32(2.0) / np.float32(128)))
    pos = np.arange(SEQ, dtype=np.float32)
    ang = pos[:, None] * inv_freq[None, :]
    cos = np.cos(ang).astype(np.float32).T
    sin = np.sin(ang).astype(np.float32).T
    cosT = np.concatenate([cos, cos], axis=0)
    sinT = np.concatenate([-sin, sin], axis=0)
    ident = np.eye(128, dtype=np.float32)
    perm = np.zeros((128, 128), np.float32)
    for p in range(128):
        perm[(p + 64) % 128, p] = 1.0
    tri = (np.arange(128)[None, :] >= np.arange(128)[:, None]).astype(np.float32)
    return cosT, sinT, ident, perm, tri


def kernel(x, mem, norm_mix_g, w_in, sgu_ln_g, sgu_ln_b, w_sgu, b_sgu, w_branch_a, w_branch_b, w_out,
           norm_xattn_g, norm_mem_g, w_xq, w_xkv, w_xo, norm_ffn_g, w_ff_gate, w_ff_up, w_ff_down, norm_final_g):
    global _PROG
    f = lambda a: np.ascontiguousarray(np.asarray(a, dtype=np.float32))
    x = f(x)
    mem = f(mem)
    cosT, sinT, ident, perm, tri = _host_consts()
    shared = {
        "g_mix": f(norm_mix_g).reshape(1, D), "g_xat": f(norm_xattn_g).reshape(1, D),
        "g_mem": f(norm_mem_g).reshape(1, D), "g_ffn": f(norm_ffn_g).reshape(1, D),
        "g_fin": f(norm_final_g).reshape(1, D),
        "w_in": f(w_in)[0], "w_pa": f(w_branch_a)[0], "w_pb": f(w_branch_b)[0], "w_out": f(w_out)[0],
        "w_xq": f(w_xq)[0], "w_xkv": f(w_xkv)[0], "w_xo": f(w_xo)[0],
        "w_fg": f(w_ff_gate)[0], "w_fu": f(w_ff_up)[0], "w_fd": f(w_ff_down)[0],
        "ln_g": f(sgu_ln_g).reshape(1, 2048), "ln_b": f(sgu_ln_b).reshape(1, 2048),
        "w_sT": np.ascontiguousarray(np.transpose(f(w_sgu)[0], (2, 0, 1))).reshape(128, 16 * 128),
        "b_s": f(b_sgu)[0].reshape(1, 16 * 128),
        "tri": tri, "ident": ident, "perm": perm,
        "cos_all": cosT, "sin_all": sinT,
    }
    in_maps = []
    own_idx = []
    for c in range(8):
        b, j = c // 4, c % 4
        blocks = [4 * i + j for i in range(8)]
        tok = np.concatenate([np.arange(g * BLK, (g + 1) * BLK) for g in blocks])
        own_idx.append((b, tok))
        negb = np.zeros((8, 32), np.float32)
        validm = np.zeros((8, 32), np.float32)
        ownm = np.zeros((8, 32), np.float32)
        for i in range(8):
            cur = 4 * i + j
            negb[i, cur:] = NEG
            validm[i, :cur] = 1.0
            ownm[i, cur] = 1.0
        dm = np.zeros((128, 4, 2, 256), np.float32)
        kk = np.arange(128)
        q = np.arange(256)
        for cc in range(4):
            for ch in range(2):
                if cc < j:
                    dm[:, cc, ch, :] = 1.0
                elif cc == j:
                    dm[:, cc, ch, :] = ((ch * 128 + kk)[:, None] <= q[None, :]).astype(np.float32)
        m = dict(shared)
        m.update({
            "x_own": np.ascontiguousarray(x[b][tok]),
            "x_all": x[b],
            "mem_b": mem[b],
            "cos_own": np.ascontiguousarray(cosT[:, tok]),
            "sin_own": np.ascontiguousarray(sinT[:, tok]),
            "negb": negb.reshape(1, 256), "validm": validm.reshape(1, 256), "ownm": ownm.reshape(1, 256),
            "dmask": dm.reshape(128, 4 * 512),
        })
        in_maps.append(m)
    if _PROG is None:
        _PROG = build_program()
    res = run_bass_kernel_spmd(_PROG, in_maps, core_ids=list(range(8)))
    outp = np.empty((NB, SEQ, D), np.float32)
    for c in range(8):
        b, tok = own_idx[c]
        outp[b, tok] = res.results[c]["out"]
    return outp
```

```python
import math
from contextlib import ExitStack

import numpy as np
import concourse.bass as bass
import concourse.mybir as mybir
from concourse.bass_utils import run_bass_kernel_spmd

F32 = mybir.dt.float32
BF16 = mybir.dt.bfloat16
AF = mybir.ActivationFunctionType
ALU = mybir.AluOpType
AX = mybir.AxisListType

D = 4096
KC = D // 128
SEQ = 8192
NB = 2
HEADS = 16
BLK = 256
NBLK = SEQ // BLK
OWN = 2048
T = 512
NT_OWN = OWN // T
NT_ALL = SEQ // T
NMEM = 256
XH = 4
DFF = 11008
FC = DFF // 128
IN_COLS = 18432
C_Q, C_K, C_V, C_U, C_VS, C_GA, C_GB = 0, 2048, 4096, 6144, 8192, 10240, 14336
RMS_EPS = 1e-6
LN_EPS = 1e-5
SCALE = 1.0 / math.sqrt(128.0)
NEG = -1e30


class SemC:
    def __init__(self, h):
        self.h = h
        self.v = 0


class Buf:
    def __init__(self, name="", dram=False, psum=False):
        self.name = name
        self.dram = dram
        self.psum = psum
        self.w = None
        self.r = {}
        self.lsem = None
        self.ssem = None


class Eng:
    def __init__(self, eng, semc, name):
        self.e = eng
        self.s = semc
        self.name = name
        self.seen = {}

    def wait(self, tok):
        s, v = tok
        if self.seen.get(id(s), 0) < v:
            self.e.wait_ge(s.h, v)
            self.seen[id(s)] = v


class K:
    def __init__(self, nc):
        self.nc = nc
        self.top = ExitStack()
        self.free_sems = []
        for i in range(96):
            self.free_sems.append(SemC(self.top.enter_context(nc.semaphore(f"s{i}"))))
        self.PE = Eng(nc.tensor, self.free_sems.pop(), "pe")
        self.ACT = Eng(nc.scalar, self.free_sems.pop(), "act")
        self.DVE = Eng(nc.vector, self.free_sems.pop(), "dve")
        self.POOL = Eng(nc.gpsimd, self.free_sems.pop(), "pool")
        self.SP = Eng(nc.sync, self.free_sems.pop(), "sp")
        self.engs = [self.PE, self.ACT, self.DVE, self.POOL, self.SP]
        self.dma_toks = {}
        self.phase_sems = []
        self.phase_sems_pool = []
        self.free_sems_pool = [self.free_sems.pop() for _ in range(16)]
        self.uid = 0

    def sb(self, st, shape, dt, name=None):
        self.uid += 1
        t = st.enter_context(self.nc.sbuf_tensor(f"{name or 'sb'}_{self.uid}", list(shape), dt))
        b = Buf(name)
        return t, b

    def ps(self, st, shape, dt, name=None):
        self.uid += 1
        t = st.enter_context(self.nc.psum_tensor(f"{name or 'ps'}_{self.uid}", list(shape), dt))
        return t, Buf(name, psum=True)

    def getsem(self, Q):
        if Q is self.POOL:
            s = self.free_sems_pool.pop()
            self.phase_sems_pool.append(s)
        else:
            s = self.free_sems.pop()
            self.phase_sems.append(s)
        return s

    def end_phase(self):
        self.barrier()
        self.free_sems.extend(self.phase_sems)
        self.phase_sems = []
        self.free_sems_pool.extend(self.phase_sems_pool)
        self.phase_sems_pool = []

    def barrier(self):
        toks = [(e.s, e.s.v) for e in self.engs] + list(self.dma_toks.values())
        for e in self.engs:
            for tk in toks:
                if tk[0] is e.s:
                    continue
                e.wait(tk)
        self.dma_toks = {}

    def op(self, E, fn, reads=(), writes=(), inc=True):
        toks = []
        for b in reads:
            if b.w is not None:
                toks.append(b.w)
            if b.psum:
                toks.extend(tk for tk in b.r.values() if tk[0] is not E.s)
        for b in writes:
            if b.w is not None:
                toks.append(b.w)
            toks.extend(b.r.values())
        for tk in toks:
            if E is self.PE and tk[0] is E.s:
                continue
            E.wait(tk)
        ins = fn()
        if inc:
            E.s.v += 1
            ins.then_inc(E.s.h, 1)
            tok = (E.s, E.s.v)
        else:
            tok = (E.s, E.s.v + 1)
        for b in writes:
            b.w = tok
            b.r = {}
        for b in reads:
            b.r[id(tok[0])] = tok
        return ins

    def dma(self, Q, out, in_, sembuf, kind, reads=(), writes=()):
        if kind == "l":
            if sembuf.lsem is None:
                sembuf.lsem = self.getsem(Q)
            s = sembuf.lsem
        else:
            if sembuf.ssem is None:
                sembuf.ssem = self.getsem(Q)
            s = sembuf.ssem
        reads = [b for b in reads if not b.dram]
        writes = [b for b in writes if not b.dram]
        toks = []
        for b in reads:
            if b.w is not None:
                toks.append(b.w)
        for b in writes:
            if b.w is not None and b.w[0] is not s:
                toks.append(b.w)
            toks.extend(b.r.values())
        for tk in toks:
            Q.wait(tk)
        Q.e.dma_start(out=out, in_=in_).then_inc(s.h, 16)
        s.v += 16
        tok = (s, s.v)
        self.dma_toks[id(s)] = tok
        for b in writes:
            b.w = tok
            b.r = {}
        for b in reads:
            b.r[id(s)] = tok
        return tok


class WStream:
    def __init__(self, k, st, nslots, slot_elems, live=1):
        self.k = k
        self.slots = [k.sb(st, [128, slot_elems], BF16, "wslot") for _ in range(nslots)]
        self.reqs = []
        self.issued = 0
        self.taken = 0
        self.depth = nslots - live

    def plan(self, reqs):
        self.reqs = self.reqs + list(reqs)

    def _issue(self, i):
        ap, k0, nk, c0, ncols = self.reqs[i]
        t, b = self.slots[i % len(self.slots)]
        dst = t[:, 0:nk * ncols].rearrange("p (k c) -> p k c", k=nk)
        step = 8
        for ka in range(0, nk, step):
            kb = min(nk, ka + step)
            src = ap[(k0 + ka) * 128:(k0 + kb) * 128, c0:c0 + ncols].rearrange("(k p) c -> p k c", p=128)
            self.k.dma(self.k.POOL, dst[:, ka:kb, :], src, b, "l", writes=[b])

    def next(self):
        i = self.taken
        while self.issued <= min(i + self.depth, len(self.reqs) - 1):
            self._issue(self.issued)
            self.issued += 1
        self.taken += 1
        ap, k0, nk, c0, ncols = self.reqs[i]
        t, b = self.slots[i % len(self.slots)]
        return t[:, 0:nk * ncols].rearrange("p (k c) -> p k c", k=nk), b


class WRes:
    def __init__(self, ws, n):
        self.tiles = [ws.next() for _ in range(n)]
        self.i = 0

    def next(self):
        t = self.tiles[self.i % len(self.tiles)]
        self.i += 1
        return t


def build_program(stop_after=None, dbg=(), skip=()):
    nc = bass.Bass("TRN2", target_bir_lowering=False)
    phases_done = []

    def din(name, shape, dt=F32):
        return nc.dram_tensor(name, list(shape), dt, kind="ExternalInput").ap()

    def dscr(name, shape, dt):
        return nc.dram_tensor(name, list(shape), dt, kind=("ExternalOutput" if name in dbg else "Internal")).ap()

    x_own = din("x_own", [OWN, D])
    x_all = din("x_all", [SEQ, D])
    mem_b = din("mem_b", [NMEM, D])
    g_mix = din("g_mix", [1, D])
    g_xat = din("g_xat", [1, D])
    g_mem = din("g_mem", [1, D])
    g_ffn = din("g_ffn", [1, D])
    g_fin = din("g_fin", [1, D])
    w_in = din("w_in", [D, IN_COLS])
    w_pa = din("w_pa", [2048, D])
    w_pb = din("w_pb", [2048, D])
    w_out = din("w_out", [D, D])
    w_xq = din("w_xq", [D, 512])
    w_xkv = din("w_xkv", [D, 1024])
    w_xo = din("w_xo", [512, D])
    w_fg = din("w_fg", [D, DFF])
    w_fu = din("w_fu", [D, DFF])
    w_fd = din("w_fd", [DFF, D])
    ln_g = din("ln_g", [1, 2048])
    ln_b = din("ln_b", [1, 2048])
    w_sT = din("w_sT", [128, 16 * 128])
    b_s = din("b_s", [1, 16 * 128])
    tri = din("tri", [128, 128])
    ident = din("ident", [128, 128])
    perm = din("perm", [128, 128])
    cos_all = din("cos_all", [128, SEQ])
    sin_all = din("sin_all", [128, SEQ])
    cos_own = din("cos_own", [128, OWN])
    sin_own = din("sin_own", [128, OWN])
    negb = din("negb", [1, 8 * 32])
    validm = din("validm", [1, 8 * 32])
    ownm = din("ownm", [1, 8 * 32])
    dmask = din("dmask", [128, 4 * 512])
    out = nc.dram_tensor("out", [OWN, D], F32, kind="ExternalOutput").ap()

    hT_all = dscr("hT_all", [NT_ALL, 128, KC * T], BF16)
    hT_own = dscr("hT_own", [NT_OWN, 128, KC * T], BF16)
    KT_scr = dscr("KT_scr", [HEADS, 128, SEQ], BF16)
    V_scr = dscr("V_scr", [HEADS, 128, 64 * 132], BF16)
    QT_scr = dscr("QT_scr", [NT_OWN, 128, HEADS * T], BF16)
    obT_scr = dscr("obT_scr", [NT_OWN, 128, 16 * T], BF16)
    oaT_scr = dscr("oaT_scr", [NT_OWN, 128, 16 * T], BF16)
    x1_scr = dscr("x1_scr", [OWN, D], F32)
    x2_scr = dscr("x2_scr", [OWN, D], F32)
    x3_scr = dscr("x3_scr", [OWN, D], F32)
    h2T_scr = dscr("h2T_scr", [NT_OWN, 128, KC * T], BF16)
    h3T_scr = dscr("h3T_scr", [NT_OWN, 128, KC * T], BF16)
    memT_scr = dscr("memT_scr", [1, 128, KC * NMEM], BF16)

    k = K(nc)

    def pbc(a):
        r = a[0].partition_broadcast(128)
        assert len(r.shape) == 2, r.shape
        return r
    PE, ACT, DVE, POOL, SP = k.PE, k.ACT, k.DVE, k.POOL, k.SP

    b_hT_all = [Buf(dram=True) for _ in range(NT_ALL)]
    b_hT_own = [Buf(dram=True) for _ in range(NT_OWN)]
    b_KT = Buf(dram=True)
    b_V = Buf(dram=True)
    b_QT = [Buf(dram=True) for _ in range(NT_OWN)]
    b_obT = [Buf(dram=True) for _ in range(NT_OWN)]
    b_oaT = [Buf(dram=True) for _ in range(NT_OWN)]
    b_x1 = [Buf(dram=True) for _ in range(NT_OWN)]
    b_x2 = [Buf(dram=True) for _ in range(NT_OWN)]
    b_x3 = [Buf(dram=True) for _ in range(NT_OWN)]
    b_h2T = [Buf(dram=True) for _ in range(NT_OWN)]
    b_h3T = [Buf(dram=True) for _ in range(NT_OWN)]
    b_memT = [Buf(dram=True)]

    kmT_f, b_kmf = k.sb(k.top, [128, HEADS, NBLK], F32, "kmf")
    kmT_b, b_kmb = k.sb(k.top, [128, HEADS, NBLK], BF16, "kmb")
    identb, b_id = k.sb(k.top, [128, 128], BF16, "ident")
    permb, b_pm = k.sb(k.top, [128, 128], BF16, "perm")
    k.dma(POOL, identb[:], ident, b_id, "l", writes=[b_id])
    k.dma(POOL, permb[:], perm, b_pm, "l", writes=[b_pm])

    rr = {"ps": 0}

    def phase(name):
        if name in skip:
            return
        with ExitStack() as st_:
            yield st_

    def norm_phase(src, n_tok, gain, dst_scr, dst_bufs, src_bufs=None, tile_tok=T, final_out=None):
        with ExitStack() as st:
            gB, b_g = k.sb(st, [128, D], F32, "gB")
            k.dma(SP, gB[:], pbc(gain), b_g, "l", writes=[b_g])
            xs = [k.sb(st, [128, D], F32, "xs") for _ in range(3)]
            junk, b_junk = k.sb(st, [128, D], BF16, "junk")
            stat = [k.sb(st, [128, 2], F32, "stat") for _ in range(3)]
            if final_out is None:
                hb = [k.sb(st, [128, D], BF16, "hb") for _ in range(2)]
                hTs = [[k.sb(st, [128, 8, tile_tok], BF16, "hTs") for _ in range(4)] for _ in range(2)]
                psT = [k.ps(st, [128, 1024], BF16, "psT") for _ in range(4)]
            else:
                ob = [k.sb(st, [128, D], F32, "ob") for _ in range(2)]
            nsub = n_tok // 128
            spt = tile_tok // 128
            cnt = 0
            for s in range(nsub):
                ti = s // spt
                so = (s % spt) * 128
                xt, xb = xs[s % 3]
                rd = [src_bufs[(s * 128) // T]] if src_bufs is not None else []
                k.dma(SP, xt[:], src[s * 128:(s + 1) * 128, :], xb, "l", reads=rd, writes=[xb])
                stt, stb = stat[s % 3]
                k.op(ACT, lambda: nc.scalar.activation(out=junk[:], in_=xt[:], func=AF.Square,
                                                       accum_out=stt[:, 0:1]),
                     reads=[xb], writes=[b_junk, stb])
                k.op(DVE, lambda: nc.vector.tensor_scalar(out=stt[:, 1:2], in0=stt[:, 0:1], scalar1=1.0 / D,
                                                          scalar2=RMS_EPS, op0=ALU.mult, op1=ALU.add),
                     reads=[stb], writes=[stb])
                k.op(ACT, lambda: nc.scalar.activation(out=stt[:, 1:2], in_=stt[:, 1:2], func=AF.Sqrt),
                     reads=[stb], writes=[stb])
                k.op(DVE, lambda: nc.vector.reciprocal(out=stt[:, 1:2], in_=stt[:, 1:2]),
                     reads=[stb], writes=[stb])
                if final_out is not None:
                    ot, obb = ob[s % 2]
                    k.op(DVE, lambda: nc.vector.scalar_tensor_tensor(out=ot[:], in0=xt[:], scalar=stt[:, 1:2],
                                                                     in1=gB[:], op0=ALU.mult, op1=ALU.mult),
                         reads=[xb, stb, b_g], writes=[obb])
                    k.dma(SP, final_out[s * 128:(s + 1) * 128, :], ot[:], obb, "s", reads=[obb])
                    continue
                ht, hbb = hb[s % 2]
                k.op(DVE, lambda: nc.vector.scalar_tensor_tensor(out=ht[:], in0=xt[:], scalar=stt[:, 1:2],
                                                                 in1=gB[:], op0=ALU.mult, op1=ALU.mult),
                     reads=[xb, stb, b_g], writes=[hbb])
                for q in range(4):
                    hs, hsb = hTs[ti % 2][q]
                    pt, pb = psT[cnt % 4]
                    for j in range(8):
                        kk = q * 8 + j
                        k.op(PE, lambda: nc.tensor.transpose(out=pt[:, j * 128:(j + 1) * 128],
                                                             in_=ht[:, kk * 128:(kk + 1) * 128],
                                                             identity=identb[:]),
                             reads=[hbb, b_id], writes=[pb], inc=(j == 7))
                    srcv = pt[:, :].rearrange("p (j t) -> p j t", j=8)
                    dstv = hs[:, :, so:so + 128]
                    if cnt % 2 == 0:
                        k.op(ACT, lambda: nc.scalar.copy(out=dstv, in_=srcv), reads=[pb], writes=[hsb])
                    else:
                        k.op(DVE, lambda: nc.vector.tensor_copy(out=dstv, in_=srcv), reads=[pb], writes=[hsb])
                    cnt += 1
                if (s % spt) == spt - 1:
                    for q in range(4):
                        hs, hsb = hTs[ti % 2][q]
                        k.dma(SP, dst_scr[ti].rearrange("p (k t) -> p k t", k=KC)[:, q * 8:(q + 1) * 8, :], hs[:], hsb, "s",
                              reads=[hsb], writes=[dst_bufs[ti]])
            k.end_phase()

    def rotary_epi(P, pb, cs, csb, sn, snb, tmp, ps2, outbf, outb, kr_out=None, after=None):
        tmp["i"] = tmp.get("i", 0) + 1
        kb_t, kb_b = tmp["kb"][tmp["i"] % 2]
        t1_t, t1_b = tmp["t1"]
        t2_t, t2_b = tmp["t2"]
        p2, p2b = ps2
        k.op(ACT, lambda: nc.scalar.copy(out=kb_t[:], in_=P), reads=[pb], writes=[kb_b])

        def finish():
            k.op(PE, lambda: nc.tensor.matmul(p2[:, 0:T], permb[:], kb_t[:], start=True, stop=True),
                 reads=[kb_b, b_pm], writes=[p2b])
            k.op(DVE, lambda: nc.vector.tensor_tensor(out=t1_t[:], in0=P, in1=cs, op=ALU.mult),
                 reads=[pb, csb], writes=[t1_b])
            k.op(DVE, lambda: nc.vector.tensor_tensor(out=t2_t[:], in0=p2[:, 0:T], in1=sn, op=ALU.mult),
                 reads=[p2b, snb], writes=[t2_b])
            if kr_out is None:
                k.op(DVE, lambda: nc.vector.tensor_tensor(out=outbf, in0=t1_t[:], in1=t2_t[:], op=ALU.add),
                     reads=[t1_b, t2_b], writes=[outb])
            else:
                kr_t, kr_b = tmp["kr"]
                k.op(DVE, lambda: nc.vector.tensor_tensor(out=kr_t[:], in0=t1_t[:], in1=t2_t[:], op=ALU.add),
                     reads=[t1_b, t2_b], writes=[kr_b])
                k.op(DVE, lambda: nc.vector.reduce_sum(out=kr_out, in_=kr_t[:, :].rearrange("p (b t) -> p b t", b=2),
                                                       axis=AX.X),
                     reads=[kr_b], writes=[b_kmf])
                k.op(ACT, lambda: nc.scalar.copy(out=outbf, in_=kr_t[:]), reads=[kr_b], writes=[outb])
            if after is not None:
                after()
        return finish

    def gelu_tanh(P, pb, shape_fn, tmp, outap, outb):
        a_t, a_b = tmp["ga"]
        b_t, b_b = tmp["gb"]
        av = shape_fn(a_t)
        bv = shape_fn(b_t)
        k.op(ACT, lambda: nc.scalar.activation(out=av, in_=P, func=AF.Square), reads=[pb], writes=[a_b])
        k.op(DVE, lambda: nc.vector.tensor_scalar(out=av, in0=av, scalar1=0.044715, scalar2=1.0,
                                                  op0=ALU.mult, op1=ALU.add), reads=[a_b], writes=[a_b])
        k.op(DVE, lambda: nc.vector.tensor_tensor(out=bv, in0=av, in1=P, op=ALU.mult),
             reads=[a_b, pb], writes=[b_b])
        k.op(ACT, lambda: nc.scalar.activation(out=av, in_=bv, func=AF.Sigmoid, scale=1.5957691216057308),
             reads=[b_b], writes=[a_b])
        k.op(DVE, lambda: nc.vector.tensor_tensor(out=outap, in0=av, in1=P, op=ALU.mult),
             reads=[a_b, pb], writes=[outb])

    def gemm_acc(P, pb, nk, lhs_fn, rhs_fn, rd):
        for kk in range(nk):
            k.op(PE, lambda: nc.tensor.matmul(P, lhs_fn(kk), rhs_fn(kk), start=(kk == 0), stop=(kk == nk - 1)),
                 reads=rd, writes=[pb], inc=(kk == nk - 1))

    if "N1" not in skip:
        norm_phase(x_all, SEQ, g_mix, hT_all, b_hT_all)
        norm_phase(x_own, OWN, g_mix, hT_own, b_hT_own)

    if stop_after == "N1":
        k.barrier()
        k.top.close()
        return nc
    for st in phase("A"):
        ws = WStream(k, st, 6, KC * 256, live=4)
        hTb = [k.sb(st, [128, KC, T], BF16, "hT") for _ in range(2)]
        cst = [k.sb(st, [128, T], F32, "cs") for _ in range(2)]
        snt = [k.sb(st, [128, T], F32, "sn") for _ in range(2)]
        tmp = {"kb": [k.sb(st, [128, T], BF16), k.sb(st, [128, T], BF16)], "t1": k.sb(st, [128, T], F32), "t2": k.sb(st, [128, T], F32),
               "kr": k.sb(st, [128, T], F32)}
        kst = [k.sb(st, [128, T], BF16, "kst") for _ in range(3)]
        vst = [k.sb(st, [128, 8, 4, 132], BF16, "vst") for _ in range(2)]
        for vt_, vb_ in vst:
            k.op(DVE, lambda: nc.vector.memset(vt_[:, :, :, 128:132], 1.0), writes=[vb_])
        pss = [k.ps(st, [128, 512], F32, "ps") for _ in range(6)]
        ps2 = k.ps(st, [128, 512], F32, "ps2")
        reqs = []
        for g in range(8):
            reqs.append((w_in, 0, KC, C_K + g * 256, 256))
        for g in range(8):
            reqs.append((w_in, 0, KC, C_V + g * 256, 256))
        ws.plan(reqs)
        pi = 0
        ki = 0
        pend = None
        li = [0]

        def loadA(tt, with_tables):
            ht, hbf = hTb[li[0] % 2]
            k.dma(SP, ht[:], hT_all[tt].rearrange("p (k t) -> p k t", k=KC), hbf, "l",
                  reads=[b_hT_all[tt]], writes=[hbf])
            if with_tables:
                ct, cb = cst[li[0] % 2]
                stn, sb_ = snt[li[0] % 2]
                k.dma(SP, ct[:], cos_all[:, tt * T:(tt + 1) * T], cb, "l", writes=[cb])
                k.dma(SP, stn[:], sin_all[:, tt * T:(tt + 1) * T], sb_, "l", writes=[sb_])
            li[0] += 1

        loadA(0, True)
        ui = 0
        for sg in range(4):
            Wg = [ws.next() for _ in range(4)]
            for tt in range(NT_ALL):
                ht, hbf = hTb[ui % 2]
                ct, cb = cst[ui % 2]
                stn, sb_ = snt[ui % 2]
                ui += 1
                if pend is not None:
                    pend()
                    pend = None
                if tt + 1 < NT_ALL:
                    loadA(tt + 1, sg < 2)
                elif sg + 1 < 4:
                    loadA(0, sg + 1 < 2)
                if sg < 2:
                    for gl in range(4):
                        W, wb = Wg[gl]
                        for hh in range(2):
                            h = sg * 8 + gl * 2 + hh
                            P, pb = pss[pi % 6]
                            pi += 1
                            gemm_acc(P[:, 0:T], pb, KC, lambda kk: W[:, kk, hh * 128:(hh + 1) * 128],
                                     lambda kk: ht[:, kk, :], [wb, hbf])
                            ko, kob = kst[ki % 3]
                            ki += 1
                            if pend is not None:
                                pend()

                            def store_k(h=h, tt=tt, ko=ko, kob=kob):
                                k.dma(SP, KT_scr[h][:, tt * T:(tt + 1) * T], ko[:], kob, "s", reads=[kob], writes=[b_KT])
                            pend = rotary_epi(P[:, 0:T], pb, ct[:], cb, stn[:], sb_, tmp, ps2, ko[:], kob,
                                              kr_out=kmT_f[:, h, tt * 2:(tt + 1) * 2], after=store_k)
                    if pend is not None and tt == NT_ALL - 1:
                        pend()
                        pend = None
                else:
                    for gl in range(4):
                        W, wb = Wg[gl]
                        for m in range(4):
                            P, pb = pss[pi % 6]
                            pi += 1
                            gemm_acc(P[:, 0:256], pb, KC, lambda kk: ht[:, kk, m * 128:(m + 1) * 128],
                                     lambda kk: W[:, kk, :], [wb, hbf])
                            vt, vb = vst[tt % 2]
                            dsto = vt[:, gl * 2:gl * 2 + 2, m, 0:128]
                            srco = P[:, 0:256].rearrange("p (h d) -> p h d", h=2)
                            if (gl + m) % 2 == 0:
                                k.op(ACT, lambda: nc.scalar.copy(out=dsto, in_=srco), reads=[pb], writes=[vb])
                            else:
                                k.op(DVE, lambda: nc.vector.tensor_copy(out=dsto, in_=srco), reads=[pb], writes=[vb])
                    h0 = (sg - 2) * 8
                    vt, vb = vst[tt % 2]
                    for hq in range(2):
                        dstv = V_scr[h0 + hq * 4:h0 + hq * 4 + 4].rearrange("h p cd -> p h cd")[:, :, tt * 528:(tt + 1) * 528]
                        srcv = vt[:, hq * 4:hq * 4 + 4, :, :].rearrange("p h m d -> p h (m d)")
                        k.dma(SP, dstv, srcv, vb, "s", reads=[vb], writes=[b_V])
        k.op(DVE, lambda: nc.vector.tensor_scalar(out=kmT_b[:], in0=kmT_f[:], scalar1=1.0 / BLK, scalar2=None,
                                                  op0=ALU.mult), reads=[b_kmf], writes=[b_kmb])
        k.end_phase()

    if stop_after == "A":
        k.barrier()
        k.top.close()
        return nc
    for st in phase("B2"):
        ws = WStream(k, st, 3, KC * 256)
        hTb = [k.sb(st, [128, KC, T], BF16, "hT") for _ in range(2)]
        cst = [k.sb(st, [128, T], F32, "cs") for _ in range(2)]
        snt = [k.sb(st, [128, T], F32, "sn") for _ in range(2)]
        tmp = {"kb": [k.sb(st, [128, T], BF16), k.sb(st, [128, T], BF16)], "t1": k.sb(st, [128, T], F32), "t2": k.sb(st, [128, T], F32)}
        qst = [k.sb(st, [128, HEADS, T], BF16, "qst") for _ in range(2)]
        pss = [k.ps(st, [128, 512], F32, "ps") for _ in range(6)]
        ps2 = k.ps(st, [128, 512], F32, "ps2")
        ws.plan([(w_in, 0, KC, C_Q + g * 256, 256) for tt in range(NT_OWN) for g in range(8)])
        pi = 0
        pend = None
        def loadB(tt):
            ht, hbf = hTb[tt % 2]
            k.dma(SP, ht[:], hT_own[tt].rearrange("p (k t) -> p k t", k=KC), hbf, "l",
                  reads=[b_hT_own[tt]], writes=[hbf])
            ct, cb = cst[tt % 2]
            stn, sb_ = snt[tt % 2]
            k.dma(SP, ct[:], cos_own[:, tt * T:(tt + 1) * T], cb, "l", writes=[cb])
            k.dma(SP, stn[:], sin_own[:, tt * T:(tt + 1) * T], sb_, "l", writes=[sb_])

        loadB(0)
        for tt in range(NT_OWN):
            ht, hbf = hTb[tt % 2]
            ct, cb = cst[tt % 2]
            stn, sb_ = snt[tt % 2]
            if tt + 1 < NT_OWN:
                loadB(tt + 1)
            qt, qb = qst[tt % 2]
            for g in range(8):
                W, wb = ws.next()
                for hh in range(2):
                    h = g * 2 + hh
                    P, pb = pss[pi % 6]
                    pi += 1
                    gemm_acc(P[:, 0:T], pb, KC, lambda kk: W[:, kk, hh * 128:(hh + 1) * 128],
                             lambda kk: ht[:, kk, :], [wb, hbf])
                    if pend is not None:
                        pend()
                    pend = rotary_epi(P[:, 0:T], pb, ct[:], cb, stn[:], sb_, tmp, ps2, qt[:, h, :], qb)
            pend()
            pend = None
            k.dma(SP, QT_scr[tt].rearrange("p (h t) -> p h t", h=HEADS), qt[:], qb, "s", reads=[qb],
                  writes=[b_QT[tt]])
        k.end_phase()

    if stop_after == "B2":
        k.barrier()
        k.top.close()
        return nc
    for st in phase("B3"):
        ws = WStream(k, st, 3, KC * 256)
        hTb = [k.sb(st, [128, KC, T], BF16, "hT") for _ in range(1)]
        guT, b_gu = k.sb(st, [128, 16, T], BF16, "guT")
        vsb = [k.sb(st, [128, 2048], F32, "vsb") for _ in range(4)]
        vln = [k.sb(st, [128, 2048], BF16, "vln") for _ in range(2)]
        lngB, b_lng = k.sb(st, [128, 2048], F32, "lng")
        lnbB, b_lnb = k.sb(st, [128, 2048], F32, "lnb")
        bsB, b_bs = k.sb(st, [128, 16, 128], F32, "bsB")
        wTm, b_wT = k.sb(st, [128, 16, 128], BF16, "wTm")
        trif, b_tri = k.sb(st, [128, 128], F32, "tri")
        obst = [k.sb(st, [128, 16, T], BF16, "obst") for _ in range(1)]
        tmpA = [{"ga": k.sb(st, [128, T], F32), "gb": k.sb(st, [128, T], F32)} for _ in range(2)]
        lnt, b_lnt = k.sb(st, [128, 2048], F32, "lnt")
        wTf, b_wTf = lnt[:, :].rearrange("p (g t) -> p g t", g=16), b_lnt
        junk, b_junk = k.sb(st, [128, 2048], BF16, "junk")
        stat = [k.sb(st, [128, 8], F32, "stat") for _ in range(2)]
        mt, b_mt = k.sb(st, [128, 4, 128], F32, "mt")
        pss = [k.ps(st, [128, 512], F32, "ps") for _ in range(8)]
        k.dma(SP, lngB[:], pbc(ln_g), b_lng, "l", writes=[b_lng])
        k.dma(SP, lnbB[:], pbc(ln_b), b_lnb, "l", writes=[b_lnb])
        k.dma(SP, bsB[:], pbc(b_s).rearrange("p (g t) -> p g t", g=16), b_bs, "l",
              writes=[b_bs])
        k.dma(SP, wTf, w_sT.rearrange("p (g t) -> p g t", g=16), b_wTf, "l", writes=[b_wTf])
        k.dma(SP, trif[:], tri, b_tri, "l", writes=[b_tri])
        for g in range(16):
            k.op(DVE, lambda: nc.vector.tensor_tensor(out=wTm[:, g, :], in0=wTf[:, g, :], in1=trif[:], op=ALU.mult),
                 reads=[b_wTf, b_tri], writes=[b_wT])
        reqs = []
        for tt in range(NT_OWN):
            for g in range(8):
                reqs.append((w_in, 0, KC, C_U + g * 256, 256))
            for g in range(8):
                reqs.append((w_in, 0, KC, C_VS + g * 256, 256))
        ws.plan(reqs)
        pi = 0
        gi = 0
        ht, hbf = hTb[0]

        def loadB3(tt):
            k.dma(SP, ht[:], hT_own[tt].rearrange("p (k t) -> p k t", k=KC), hbf, "l",
                  reads=[b_hT_own[tt]], writes=[hbf])

        loadB3(0)
        for tt in range(NT_OWN):
            for g in range(8):
                W, wb = ws.next()
                for hh in range(2):
                    gg = g * 2 + hh
                    P, pb = pss[pi % 8]
                    pi += 1
                    gemm_acc(P[:, 0:T], pb, KC, lambda kk: W[:, kk, hh * 128:(hh + 1) * 128],
                             lambda kk: ht[:, kk, :], [wb, hbf])
                    gelu_tanh(P[:, 0:T], pb, lambda t: t[:, 0:T], tmpA[gi % 2], guT[:, gg, :], b_gu)
                    gi += 1
            for g in range(8):
                W, wb = ws.next()
                for m in range(4):
                    P, pb = pss[pi % 8]
                    pi += 1
                    gemm_acc(P[:, 0:256], pb, KC, lambda kk: ht[:, kk, m * 128:(m + 1) * 128],
                             lambda kk: W[:, kk, :], [wb, hbf])
                    vt, vb = vsb[m]
                    gelu_tanh(P[:, 0:256], pb, lambda t: t[:, 0:256], tmpA[gi % 2], vt[:, g * 256:(g + 1) * 256], vb)
                    gi += 1
            if tt + 1 < NT_OWN:
                loadB3(tt + 1)
            ot, obb = obst[0]
            for m in range(4):
                vt, vb = vsb[m]
                stt, stb = stat[m % 2]
                k.op(DVE, lambda: nc.vector.reduce_sum(out=stt[:, 0:1], in_=vt[:], axis=AX.X), reads=[vb], writes=[stb])
                k.op(ACT, lambda: nc.scalar.activation(out=junk[:], in_=vt[:], func=AF.Square, accum_out=stt[:, 1:2]),
                     reads=[vb], writes=[b_junk, stb])
                k.op(DVE, lambda: nc.vector.tensor_scalar(out=stt[:, 2:4], in0=stt[:, 0:2], scalar1=1.0 / 2048,
                                                          scalar2=None, op0=ALU.mult), reads=[stb], writes=[stb])
                k.op(DVE, lambda: nc.vector.tensor_tensor(out=stt[:, 4:5], in0=stt[:, 2:3], in1=stt[:, 2:3], op=ALU.mult),
                     reads=[stb], writes=[stb])
                k.op(DVE, lambda: nc.vector.tensor_tensor(out=stt[:, 5:6], in0=stt[:, 3:4], in1=stt[:, 4:5], op=ALU.subtract),
                     reads=[stb], writes=[stb])
                k.op(DVE, lambda: nc.vector.tensor_scalar(out=stt[:, 6:7], in0=stt[:, 5:6], scalar1=LN_EPS, scalar2=None,
                                                          op0=ALU.add), reads=[stb], writes=[stb])
                k.op(ACT, lambda: nc.scalar.activation(out=stt[:, 6:7], in_=stt[:, 6:7], func=AF.Sqrt),
                     reads=[stb], writes=[stb])
                k.op(DVE, lambda: nc.vector.reciprocal(out=stt[:, 6:7], in_=stt[:, 6:7]),
                     reads=[stb], writes=[stb])
                k.op(DVE, lambda: nc.vector.tensor_scalar(out=lnt[:], in0=vt[:], scalar1=stt[:, 2:3], scalar2=stt[:, 6:7],
                                                          op0=ALU.subtract, op1=ALU.mult), reads=[vb, stb], writes=[b_lnt])
                k.op(DVE, lambda: nc.vector.tensor_tensor(out=lnt[:], in0=lnt[:], in1=lngB[:], op=ALU.mult),
                     reads=[b_lnt, b_lng], writes=[b_lnt])
                vl, vlb = vln[m % 2]
                k.op(DVE, lambda: nc.vector.tensor_tensor(out=vl[:], in0=lnt[:], in1=lnbB[:], op=ALU.add),
                     reads=[b_lnt, b_lnb], writes=[vlb])
                for gq in range(4):
                    P, pb = pss[pi % 8]
                    pi += 1
                    for gg in range(4):
                        g = gq * 4 + gg
                        k.op(PE, lambda: nc.tensor.matmul(P[:, gg * 128:(gg + 1) * 128], vl[:, g * 128:(g + 1) * 128],
                                                          wTm[:, g, :], start=True, stop=True),
                             reads=[vlb, b_wT], writes=[pb], inc=(gg == 3))
                    k.op(DVE, lambda: nc.vector.tensor_tensor(out=mt[:], in0=P[:, :].rearrange("p (g t) -> p g t", g=4),
                                                              in1=bsB[:, gq * 4:(gq + 1) * 4, :], op=ALU.add),
                         reads=[pb, b_bs], writes=[b_mt])
                    k.op(DVE, lambda: nc.vector.tensor_tensor(out=ot[:, gq * 4:(gq + 1) * 4, m * 128:(m + 1) * 128],
                                                              in0=mt[:], in1=guT[:, gq * 4:(gq + 1) * 4, m * 128:(m + 1) * 128],
                                                              op=ALU.mult),
                         reads=[b_mt, b_gu], writes=[obb])
            k.dma(SP, obT_scr[tt].rearrange("p (g t) -> p g t", g=16), ot[:], obb, "s", reads=[obb],
                  writes=[b_obT[tt]])
        k.end_phase()

    if stop_after == "B3":
        k.barrier()
        k.top.close()
        return nc
    for st in phase("C"):
        qtb = [k.sb(st, [128, HEADS, T], BF16, "qt") for _ in range(2)]
        kbuf = [k.sb(st, [128, SEQ], BF16, "kbuf") for _ in range(2)]
        vbuf = [k.sb(st, [128, 64, 132], BF16, "vbuf") for _ in range(2)]
        negB, b_neg = k.sb(st, [128, 8, 32], F32, "negB")
        valB, b_val = k.sb(st, [128, 8, 32], F32, "valB")
        ownB, b_own = k.sb(st, [128, 8, 32], F32, "ownB")
        dmk, b_dmk = k.sb(st, [128, 4, 512], BF16, "dmk")
        gms = [k.sb(st, [128, 2, 32], F32, "gms") for _ in range(4)]
        m8 = [k.sb(st, [128, 16], F32, "m8") for _ in range(4)]
        selA, b_selA = k.sb(st, [128, HEADS * 2, 2, 32], F32, "selA")
        ptb = [k.sb(st, [128, 512], BF16, "pt") for _ in range(4)]
        acc = [k.sb(st, [128, 2, 132], F32, "acc") for _ in range(2)]
        rden = [k.sb(st, [128, 2], F32, "rden") for _ in range(2)]
        oat, b_oat = k.sb(st, [128, 4, 2048], BF16, "oat")
        oast = [k.sb(st, [128, 16, T], BF16, "oast") for _ in range(1)]
        psS = [k.ps(st, [128, 512], F32, "psS") for _ in range(3)]
        psO = [k.ps(st, [128, 512], F32, "psO") for _ in range(2)]
        psG = [k.ps(st, [128, 512], F32, "psG") for _ in range(1)]
        psT = [k.ps(st, [128, 1024], BF16, "psT") for _ in range(2)]
        k.dma(SP, negB[:], pbc(negb).rearrange("p (i n) -> p i n", i=8), b_neg, "l", writes=[b_neg])
        k.dma(SP, valB[:], pbc(validm).rearrange("p (i n) -> p i n", i=8), b_val, "l", writes=[b_val])
        k.dma(SP, ownB[:], pbc(ownm).rearrange("p (i n) -> p i n", i=8), b_own, "l", writes=[b_own])
        k.dma(POOL, dmk[:], dmask.rearrange("p (c q) -> p c q", c=4), b_dmk, "l", writes=[b_dmk])
        si = 0
        oi = 0
        gi = 0
        pti = 0
        for tt in range(NT_OWN):
            qt, qb = qtb[tt % 2]
            if tt == 0:
                k.dma(SP, qt[:], QT_scr[0].rearrange("p (h t) -> p h t", h=HEADS), qb, "l", reads=[b_QT[0]], writes=[qb])
            if tt + 1 < NT_OWN:
                qn, qnb = qtb[(tt + 1) % 2]
                k.dma(SP, qn[:], QT_scr[tt + 1].rearrange("p (h t) -> p h t", h=HEADS), qnb, "l", reads=[b_QT[tt + 1]],
                      writes=[qnb])
            nkb = 8 * tt + 8
            for h in range(HEADS):
                Pg, pgb = psG[0]
                for li in range(2):
                    q0 = li * 256
                    for s in range(2):
                        k.op(PE, lambda: nc.tensor.matmul(Pg[:, (li * 2 + s) * 32:(li * 2 + s + 1) * 32],
                                                          qt[:, h, q0 + s * 128:q0 + (s + 1) * 128],
                                                          kmT_b[:, h, :], start=True, stop=True),
                             reads=[qb, b_kmb], writes=[pgb], inc=(li == 1 and s == 1))
                for li in range(2):
                    i = 2 * tt + li
                    gt, gb_ = gms[gi % 4]
                    m8t, m8b = m8[gi % 4]
                    gi += 1
                    k.op(DVE, lambda: nc.vector.tensor_tensor(
                        out=gt[:], in0=Pg[:, li * 64:(li + 1) * 64].rearrange("p (s n) -> p s n", s=2),
                        in1=negB[:, i:i + 1, :].to_broadcast([128, 2, 32]), op=ALU.add),
                         reads=[pgb, b_neg], writes=[gb_])
                    for s in range(2):
                        k.op(DVE, lambda: nc.vector.max(out=m8t[:, s * 8:(s + 1) * 8], in_=gt[:, s, :]),
                             reads=[gb_], writes=[m8b])
                    for s in range(2):
                        k.op(DVE, lambda: nc.vector.scalar_tensor_tensor(out=selA[:, h * 2 + li, s, :], in0=gt[:, s, :],
                                                                         scalar=m8t[:, s * 8 + 2:s * 8 + 3],
                                                                         in1=valB[:, i, :], op0=ALU.is_ge, op1=ALU.mult),
                             reads=[gb_, m8b, b_val], writes=[b_selA])
            for li in range(2):
                i = 2 * tt + li
                k.op(DVE, lambda: nc.vector.tensor_tensor(
                    out=selA[:, li::2, :, :], in0=selA[:, li::2, :, :],
                    in1=ownB[:, i:i + 1, :].unsqueeze(1).to_broadcast([128, HEADS, 2, 32]), op=ALU.add),
                     reads=[b_selA, b_own], writes=[b_selA])
            for h in range(HEADS):
                kt, kbb = kbuf[h % 2]
                vt, vb = vbuf[h % 2]
                k.dma(SP, kt[:, 0:nkb * BLK], KT_scr[h][:, 0:nkb * BLK], kbb, "l", reads=[b_KT], writes=[kbb])
                vsrc = V_scr[h].rearrange("p (c d) -> p c d", c=64)
                k.dma(SP, vt[:, 0:2 * nkb, :], vsrc[:, 0:2 * nkb, :], vb, "l", reads=[b_V], writes=[vb])
                for li in range(2):
                    i = 2 * tt + li
                    q0 = li * 256
                    at, ab = acc[li]
                    nblk = 4 * i + 4

                    def qk(n):
                        nonlocal si
                        Ps, psb = psS[si % 3]
                        si += 1
                        for c2 in range(2):
                            k.op(PE, lambda: nc.tensor.matmul(Ps[:, c2 * 256:(c2 + 1) * 256],
                                                              kt[:, n * 256 + c2 * 128:n * 256 + (c2 + 1) * 128],
                                                              qt[:, h, q0:q0 + 256], start=True, stop=True),
                                 reads=[kbb, qb], writes=[psb], inc=(c2 == 1))
                        return Ps, psb

                    cur = qk(0)
                    for n in range(nblk):
                        Ps, psb = cur
                        if n + 1 < nblk:
                            cur = qk(n + 1)
                        pt, ptbb = ptb[pti % 4]
                        pti += 1
                        k.op(ACT, lambda: nc.scalar.activation(out=pt[:], in_=Ps[:, :], func=AF.Exp, scale=SCALE),
                             reads=[psb], writes=[ptbb])
                        if n >= 4 * i:
                            c = n - 4 * i
                            k.op(DVE, lambda: nc.vector.tensor_tensor(out=pt[:], in0=pt[:], in1=dmk[:, c, :], op=ALU.mult),
                                 reads=[ptbb, b_dmk], writes=[ptbb])
                        Po, pob = psO[oi % 2]
                        oi += 1
                        for s in range(2):
                            for c2 in range(2):
                                k.op(PE, lambda: nc.tensor.matmul(Po[:, s * 132:s * 132 + 129],
                                                                  pt[:, c2 * 256 + s * 128:c2 * 256 + (s + 1) * 128],
                                                                  vt[:, n * 2 + c2, 0:129], start=(c2 == 0), stop=(c2 == 1)),
                                     reads=[ptbb, vb], writes=[pob], inc=(s == 1 and c2 == 1))
                        for s in range(2):
                            selv = selA[:, h * 2 + li, s, n:n + 1]
                            if n == 0:
                                k.op(DVE, lambda: nc.vector.tensor_scalar(out=at[:, s, 0:129], in0=Po[:, s * 132:s * 132 + 129],
                                                                          scalar1=selv, scalar2=None, op0=ALU.mult),
                                     reads=[pob, b_selA], writes=[ab])
                            else:
                                k.op(DVE, lambda: nc.vector.scalar_tensor_tensor(out=at[:, s, 0:129],
                                                                                 in0=Po[:, s * 132:s * 132 + 129],
                                                                                 scalar=selv, in1=at[:, s, 0:129],
                                                                                 op0=ALU.mult, op1=ALU.add),
                                     reads=[pob, b_selA, ab], writes=[ab])
                    rt, rb = rden[li]
                    k.op(DVE, lambda: nc.vector.reciprocal(out=rt[:, 0:2], in_=at[:, :, 128]), reads=[ab], writes=[rb])
                    for s in range(2):
                        k.op(DVE, lambda: nc.vector.tensor_scalar(out=oat[:, li * 2 + s, h * 128:(h + 1) * 128],
                                                                  in0=at[:, s, 0:128], scalar1=rt[:, s:s + 1],
                                                                  scalar2=None, op0=ALU.mult),
                             reads=[ab, rb], writes=[b_oat])
            ost, osb = oast[0]
            tci = 0
            for sub in range(4):
                for hq in range(2):
                    pT, pTb = psT[tci % 2]
                    for j in range(8):
                        hh = hq * 8 + j
                        k.op(PE, lambda: nc.tensor.transpose(out=pT[:, j * 128:(j + 1) * 128],
                                                             in_=oat[:, sub, hh * 128:(hh + 1) * 128], identity=identb[:]),
                             reads=[b_oat, b_id], writes=[pTb], inc=(j == 7))
                    srcv = pT[:, :].rearrange("p (j t) -> p j t", j=8)
                    dstv = ost[:, hq * 8:(hq + 1) * 8, sub * 128:(sub + 1) * 128]
                    if tci % 2 == 0:
                        k.op(ACT, lambda: nc.scalar.copy(out=dstv, in_=srcv), reads=[pTb], writes=[osb])
                    else:
                        k.op(DVE, lambda: nc.vector.tensor_copy(out=dstv, in_=srcv), reads=[pTb], writes=[osb])
                    tci += 1
            k.dma(SP, oaT_scr[tt].rearrange("p (h t) -> p h t", h=16), ost[:], osb, "s", reads=[osb],
                  writes=[b_oaT[tt]])
        k.end_phase()

    if stop_after == "C":
        k.barrier()
        k.top.close()
        return nc
    def resid_gemm(st_outer, ws, inT, inTb, nk, w_ap, ncg, ncols, x_src, x_src_bufs, x_dst, x_dst_bufs, tt, pss, pi,
                   xps, xos):
        for ng in range(ncg):
            W, wb = ws.next()
            xp, xpb = xps[ng % len(xps)]
            xo, xob = xos[ng % len(xos)]
            srcv = x_src[tt * T:(tt + 1) * T, ng * ncols:(ng + 1) * ncols].rearrange("(m p) c -> p m c", p=128)
            k.dma(SP, xp[:, :, 0:ncols], srcv, xpb, "l", reads=[x_src_bufs[tt]] if x_src_bufs else [], writes=[xpb])
            for m in range(4):
                P, pb = pss[pi[0] % len(pss)]
                pi[0] += 1
                gemm_acc(P[:, 0:ncols], pb, nk, lambda kk: inT[:, kk, m * 128:(m + 1) * 128], lambda kk: W[:, kk, :],
                         [wb, inTb])
                k.op(DVE, lambda: nc.vector.tensor_tensor(out=xo[:, m, 0:ncols], in0=P[:, 0:ncols], in1=xp[:, m, 0:ncols],
                                                          op=ALU.add),
                     reads=[pb, xpb], writes=[xob])
            dstv = x_dst[tt * T:(tt + 1) * T, ng * ncols:(ng + 1) * ncols].rearrange("(m p) c -> p m c", p=128)
            k.dma(SP, dstv, xo[:, :, 0:ncols], xob, "s", reads=[xob], writes=[x_dst_bufs[tt]])

    for st in phase("D"):
        ws = WStream(k, st, 3, KC * 256, live=2)
        wsP = WStream(k, st, 3, 16 * 256, live=2)
        hTb = k.sb(st, [128, KC, T], BF16, "hT")
        oaTb = k.sb(st, [128, 16, T], BF16, "oaT")
        obTb = k.sb(st, [128, 16, T], BF16, "obT")
        mT, b_mT = k.sb(st, [128, KC, T], BF16, "mT")
        sa = [k.sb(st, [128, T], F32, "sa") for _ in range(2)]
        sbb = [k.sb(st, [128, T], F32, "sb") for _ in range(2)]
        t1 = [k.sb(st, [128, T], F32, "t1") for _ in range(2)]
        t2 = [k.sb(st, [128, T], F32, "t2") for _ in range(2)]
        xps = [k.sb(st, [128, 4, 256], F32, "xp") for _ in range(2)]
        xos = [k.sb(st, [128, 4, 256], F32, "xo") for _ in range(2)]
        pss = [k.ps(st, [128, 512], F32, "ps") for _ in range(8)]
        reqs = []
        reqsP = []
        for tt in range(NT_OWN):
            for cg in range(16):
                reqs.append((w_in, 0, KC, C_GA + cg * 256, 256))
                reqs.append((w_in, 0, KC, C_GB + cg * 256, 256))
                reqsP.append((w_pa, 0, 16, cg * 256, 256))
                reqsP.append((w_pb, 0, 16, cg * 256, 256))
            for ng in range(16):
                reqs.append((w_out, 0, KC, ng * 256, 256))
        ws.plan(reqs)
        wsP.plan(reqsP)
        pi = [0]
        ei = 0
        ht, hbf = hTb
        oa, oab = oaTb
        ob_, obb = obTb

        def loadD(tt):
            k.dma(SP, ht[:], hT_own[tt].rearrange("p (k t) -> p k t", k=KC), hbf, "l", reads=[b_hT_own[tt]], writes=[hbf])
            k.dma(SP, oa[:], oaT_scr[tt].rearrange("p (k t) -> p k t", k=16), oab, "l", reads=[b_oaT[tt]], writes=[oab])
            k.dma(SP, ob_[:], obT_scr[tt].rearrange("p (k t) -> p k t", k=16), obb, "l", reads=[b_obT[tt]], writes=[obb])

        loadD(0)
        for tt in range(NT_OWN):
            for cg in range(16):
                Wga, wgab = ws.next()
                Wgb, wgbb = ws.next()
                WA, wab = wsP.next()
                WB, wbb = wsP.next()
                for cc in range(2):
                    c = cg * 2 + cc
                    Pga, pgab = pss[pi[0] % 8]
                    Pgb, pgbb = pss[(pi[0] + 1) % 8]
                    PA, pab = pss[(pi[0] + 2) % 8]
                    PB, pbb = pss[(pi[0] + 3) % 8]
                    pi[0] += 4
                    cs_ = slice(cc * 128, (cc + 1) * 128)
                    gemm_acc(Pga[:, 0:T], pgab, KC, lambda kk: Wga[:, kk, cs_], lambda kk: ht[:, kk, :], [wgab, hbf])
                    gemm_acc(Pgb[:, 0:T], pgbb, KC, lambda kk: Wgb[:, kk, cs_], lambda kk: ht[:, kk, :], [wgbb, hbf])
                    gemm_acc(PA[:, 0:T], pab, 16, lambda kk: WA[:, kk, cs_], lambda kk: oa[:, kk, :], [wab, oab])
                    gemm_acc(PB[:, 0:T], pbb, 16, lambda kk: WB[:, kk, cs_], lambda kk: ob_[:, kk, :], [wbb, obb])
                    sat, sab = sa[ei % 2]
                    sbt, sbbb = sbb[ei % 2]
                    t1t, t1b = t1[ei % 2]
                    t2t, t2b = t2[ei % 2]
                    ei += 1
                    k.op(ACT, lambda: nc.scalar.activation(out=sat[:], in_=Pga[:, 0:T], func=AF.Sigmoid),
                         reads=[pgab], writes=[sab])
                    k.op(ACT, lambda: nc.scalar.activation(out=sbt[:], in_=Pgb[:, 0:T], func=AF.Sigmoid),
                         reads=[pgbb], writes=[sbbb])
                    k.op(DVE, lambda: nc.vector.tensor_tensor(out=t1t[:], in0=sat[:], in1=PA[:, 0:T], op=ALU.mult),
                         reads=[sab, pab], writes=[t1b])
                    k.op(DVE, lambda: nc.vector.tensor_tensor(out=t2t[:], in0=sbt[:], in1=PB[:, 0:T], op=ALU.mult),
                         reads=[sbbb, pbb], writes=[t2b])
                    k.op(DVE, lambda: nc.vector.tensor_tensor(out=mT[:, c, :], in0=t1t[:], in1=t2t[:], op=ALU.add),
                         reads=[t1b, t2b], writes=[b_mT])
            if tt + 1 < NT_OWN:
                loadD(tt + 1)
            resid_gemm(st, ws, mT, b_mT, KC, w_out, 16, 256, x_own, None, x1_scr, b_x1, tt, pss, pi, xps, xos)
        k.end_phase()

    if stop_after == "D":
        k.barrier()
        k.top.close()
        return nc
    if "N2" not in skip:
        norm_phase(x1_scr, OWN, g_xat, h2T_scr, b_h2T, src_bufs=b_x1)
        norm_phase(mem_b, NMEM, g_mem, memT_scr, b_memT, tile_tok=NMEM)

    if stop_after == "N2":
        k.barrier()
        k.top.close()
        return nc
    for st in phase("E"):
        ws = WStream(k, st, 2, KC * 256)
        memT, b_mT2 = k.sb(st, [128, KC, NMEM], BF16, "memT")
        KxT, b_Kx = k.sb(st, [128, XH, NMEM], BF16, "KxT")
        Vx, b_Vx = k.sb(st, [128, 2, XH, 132], BF16, "Vx")
        hTb = k.sb(st, [128, KC, T], BF16, "hT")
        QxT, b_Qx = k.sb(st, [128, XH, T], BF16, "QxT")
        ptb = [k.sb(st, [128, T], BF16, "pt") for _ in range(4)]
        rden = [k.sb(st, [128, 2], F32, "rden") for _ in range(2)]
        oxt, b_oxt = k.sb(st, [128, 4, 512], BF16, "oxt")
        oxT, b_oxT = k.sb(st, [128, XH, T], BF16, "oxT")
        xps = [k.sb(st, [128, 4, 512], F32, "xp") for _ in range(1)]
        xos = [k.sb(st, [128, 4, 512], F32, "xo") for _ in range(1)]
        pss = [k.ps(st, [128, 512], F32, "ps") for _ in range(6)]
        psT = [k.ps(st, [128, 1024], BF16, "psT") for _ in range(2)]
        ws.plan([(w_xkv, 0, KC, g * 256, 256) for g in range(4)])
        pi = [0]
        k.dma(SP, memT[:], memT_scr[0].rearrange("p (k t) -> p k t", k=KC), b_mT2, "l", reads=[b_memT[0]], writes=[b_mT2])
        k.op(DVE, lambda: nc.vector.memset(Vx[:, :, :, 128:132], 1.0), writes=[b_Vx])
        for g in range(2):
            W, wb = ws.next()
            for hh in range(2):
                h = g * 2 + hh
                P, pb = pss[pi[0] % 6]
                pi[0] += 1
                gemm_acc(P[:, 0:NMEM], pb, KC, lambda kk: W[:, kk, hh * 128:(hh + 1) * 128], lambda kk: memT[:, kk, :],
                         [wb, b_mT2])
                k.op(ACT, lambda: nc.scalar.copy(out=KxT[:, h, :], in_=P[:, 0:NMEM]), reads=[pb], writes=[b_Kx])
        for g in range(2):
            W, wb = ws.next()
            for mc in range(2):
                P, pb = pss[pi[0] % 6]
                pi[0] += 1
                gemm_acc(P[:, 0:256], pb, KC, lambda kk: memT[:, kk, mc * 128:(mc + 1) * 128], lambda kk: W[:, kk, :],
                         [wb, b_mT2])
                k.op(ACT, lambda: nc.scalar.copy(out=Vx[:, mc, 2 * g:2 * g + 2, 0:128],
                                                 in_=P[:, 0:256].rearrange("p (h d) -> p h d", h=2)),
                     reads=[pb], writes=[b_Vx])
        pti = 0
        wsq_s = WStream(k, st, 2, KC * 256, live=2)
        wsq_s.plan([(w_xq, 0, KC, g * 256, 256) for g in range(2)])
        wsq = WRes(wsq_s, 2)
        wso_s = WStream(k, st, 8, 4 * 512, live=8)
        wso_s.plan([(w_xo, 0, 4, ng * 512, 512) for ng in range(8)])
        wso = WRes(wso_s, 8)
        ht, hbf = hTb

        def loadE(tt):
            k.dma(SP, ht[:], h2T_scr[tt].rearrange("p (k t) -> p k t", k=KC), hbf, "l", reads=[b_h2T[tt]], writes=[hbf])

        loadE(0)
        for tt in range(NT_OWN):
            for g in range(2):
                W, wb = wsq.next()
                for hh in range(2):
                    h = g * 2 + hh
                    P, pb = pss[pi[0] % 6]
                    pi[0] += 1
                    gemm_acc(P[:, 0:T], pb, KC, lambda kk: W[:, kk, hh * 128:(hh + 1) * 128], lambda kk: ht[:, kk, :],
                             [wb, hbf])
                    k.op(ACT, lambda: nc.scalar.copy(out=QxT[:, h, :], in_=P[:, 0:T]), reads=[pb], writes=[b_Qx])
            if tt + 1 < NT_OWN:
                loadE(tt + 1)
            for h in range(XH):
                pts = []
                for c2 in range(2):
                    P, pb = pss[pi[0] % 6]
                    pi[0] += 1
                    k.op(PE, lambda: nc.tensor.matmul(P[:, 0:T], KxT[:, h, c2 * 128:(c2 + 1) * 128], QxT[:, h, :],
                                                      start=True, stop=True), reads=[b_Kx, b_Qx], writes=[pb])
                    pt, ptbb = ptb[pti % 4]
                    pti += 1
                    k.op(ACT, lambda: nc.scalar.activation(out=pt[:], in_=P[:, 0:T], func=AF.Exp, scale=SCALE),
                         reads=[pb], writes=[ptbb])
                    pts.append((pt, ptbb))
                for s in range(4):
                    P, pb = pss[pi[0] % 6]
                    pi[0] += 1
                    for c2 in range(2):
                        pt, ptbb = pts[c2]
                        k.op(PE, lambda: nc.tensor.matmul(P[:, 0:129], pt[:, s * 128:(s + 1) * 128], Vx[:, c2, h, 0:129],
                                                          start=(c2 == 0), stop=(c2 == 1)),
                             reads=[ptbb, b_Vx], writes=[pb], inc=(c2 == 1))
                    rt, rb = rden[s % 2]
                    k.op(DVE, lambda: nc.vector.reciprocal(out=rt[:, 0:1], in_=P[:, 128:129]), reads=[pb], writes=[rb])
                    k.op(DVE, lambda: nc.vector.tensor_scalar(out=oxt[:, s, h * 128:(h + 1) * 128], in0=P[:, 0:128],
                                                              scalar1=rt[:, 0:1], scalar2=None, op0=ALU.mult),
                         reads=[pb, rb], writes=[b_oxt])
            for s in range(4):
                pT, pTb = psT[s % 2]
                for j in range(4):
                    k.op(PE, lambda: nc.tensor.transpose(out=pT[:, j * 128:(j + 1) * 128], in_=oxt[:, s, j * 128:(j + 1) * 128],
                                                         identity=identb[:]),
                         reads=[b_oxt, b_id], writes=[pTb], inc=(j == 3))
                k.op(ACT, lambda: nc.scalar.copy(out=oxT[:, :, s * 128:(s + 1) * 128],
                                                 in_=pT[:, 0:512].rearrange("p (j t) -> p j t", j=4)),
                     reads=[pTb], writes=[b_oxT])
            resid_gemm(st, wso, oxT, b_oxT, 4, w_xo, 8, 512, x1_scr, b_x1, x2_scr, b_x2, tt, pss, pi, xps, xos)
        k.end_phase()

    if stop_after == "E":
        k.barrier()
        k.top.close()
        return nc
    if "N3" not in skip:
        norm_phase(x2_scr, OWN, g_ffn, h3T_scr, b_h3T, src_bufs=b_x2)
    for st in phase("F"):
        ws = WStream(k, st, 3, 43 * 256, live=2)
        hTb = k.sb(st, [128, KC, T], BF16, "hT")
        actT, b_act = k.sb(st, [128, FC, T], BF16, "actT")
        sg = [k.sb(st, [128, T], F32, "sg") for _ in range(2)]
        xps = [k.sb(st, [128, 4, 256], F32, "xp") for _ in range(1)]
        xos = [k.sb(st, [128, 4, 256], F32, "xo") for _ in range(1)]
        pss = [k.ps(st, [128, 512], F32, "ps") for _ in range(8)]
        reqs = []
        for tt in range(NT_OWN):
            for fg in range(43):
                reqs.append((w_fg, 0, KC, fg * 256, 256))
                reqs.append((w_fu, 0, KC, fg * 256, 256))
            for ng in range(16):
                reqs.append((w_fd, 0, 43, ng * 256, 256))
                reqs.append((w_fd, 43, 43, ng * 256, 256))
        ws.plan(reqs)
        pi = 0
        ei = 0
        ht, hbf = hTb

        def loadF(tt):
            k.dma(SP, ht[:], h3T_scr[tt].rearrange("p (k t) -> p k t", k=KC), hbf, "l", reads=[b_h3T[tt]], writes=[hbf])

        loadF(0)
        for tt in range(NT_OWN):
            for fg in range(43):
                Wg, wgb = ws.next()
                Wu, wub = ws.next()
                for cc in range(2):
                    c = fg * 2 + cc
                    Pg, pgb = pss[pi % 8]
                    Pu, pub = pss[(pi + 1) % 8]
                    pi += 2
                    cs_ = slice(cc * 128, (cc + 1) * 128)
                    gemm_acc(Pg[:, 0:T], pgb, KC, lambda kk: Wg[:, kk, cs_], lambda kk: ht[:, kk, :], [wgb, hbf])
                    gemm_acc(Pu[:, 0:T], pub, KC, lambda kk: Wu[:, kk, cs_], lambda kk: ht[:, kk, :], [wub, hbf])
                    sgt, sgb = sg[ei % 2]
                    ei += 1
                    k.op(ACT, lambda: nc.scalar.activation(out=sgt[:], in_=Pg[:, 0:T], func=AF.Silu),
                         reads=[pgb], writes=[sgb])
                    k.op(DVE, lambda: nc.vector.tensor_tensor(out=actT[:, c, :], in0=sgt[:], in1=Pu[:, 0:T], op=ALU.mult),
                         reads=[sgb, pub], writes=[b_act])
            if tt + 1 < NT_OWN:
                loadF(tt + 1)
            for ng in range(16):
                xp, xpb = xps[0]
                xo, xob = xos[0]
                srcv = x2_scr[tt * T:(tt + 1) * T, ng * 256:(ng + 1) * 256].rearrange("(m p) c -> p m c", p=128)
                k.dma(SP, xp[:], srcv, xpb, "l", reads=[b_x2[tt]], writes=[xpb])
                Ps = [pss[(pi + m) % 8] for m in range(4)]
                pi += 4
                for half in range(2):
                    W, wb = ws.next()
                    for m in range(4):
                        P, pb = Ps[m]
                        for kk in range(43):
                            k.op(PE, lambda: nc.tensor.matmul(P[:, 0:256], actT[:, half * 43 + kk, m * 128:(m + 1) * 128],
                                                              W[:, kk, :], start=(half == 0 and kk == 0),
                                                              stop=(half == 1 and kk == 42)),
                                 reads=[wb, b_act], writes=[pb], inc=(kk == 42))
                for m in range(4):
                    P, pb = Ps[m]
                    k.op(DVE, lambda: nc.vector.tensor_tensor(out=xo[:, m, :], in0=P[:, 0:256], in1=xp[:, m, :], op=ALU.add),
                         reads=[pb, xpb], writes=[xob])
                dstv = x3_scr[tt * T:(tt + 1) * T, ng * 256:(ng + 1) * 256].rearrange("(m p) c -> p m c", p=128)
                k.dma(SP, dstv, xo[:], xob, "s", reads=[xob], writes=[b_x3[tt]])
        k.end_phase()

    if stop_after == "F":
        k.barrier()
        k.top.close()
        return nc
    if "G" not in skip:
        norm_phase(x3_scr, OWN, g_fin, None, None, src_bufs=b_x3, final_out=out)
    k.top.close()
    return nc


_PROG = None


def _host_consts():
    half = 64
    inv_freq = np.power(np.float32(10000.0), -(np.arange(half, dtype=np.float32) * np.float32(2.0) / np.float32(128)))
    pos = np.arange(SEQ, dtype=np.float32)
    ang = pos[:, None] * inv_freq[None, :]
    cos = np.cos(ang).astype(np.float32).T
    sin = np.sin(ang).astype(np.float32).T
    cosT = np.concatenate([cos, cos], axis=0)
    sinT = np.concatenate([-sin, sin], axis=0)
    ident = np.eye(128, dtype=np.float32)
    perm = np.zeros((128, 128), np.float32)
    for p in range(128):
        perm[(p + 64) % 128, p] = 1.0
    tri = (np.arange(128)[None, :] >= np.arange(128)[:, None]).astype(np.float32)
    return cosT, sinT, ident, perm, tri


def kernel(x, mem, norm_mix_g, w_in, sgu_ln_g, sgu_ln_b, w_sgu, b_sgu, w_branch_a, w_branch_b, w_out,
           norm_xattn_g, norm_mem_g, w_xq, w_xkv, w_xo, norm_ffn_g, w_ff_gate, w_ff_up, w_ff_down, norm_final_g):
    global _PROG
    f = lambda a: np.ascontiguousarray(np.asarray(a, dtype=np.float32))
    x = f(x)
    mem = f(mem)
    cosT, sinT, ident, perm, tri = _host_consts()
    shared = {
        "g_mix": f(norm_mix_g).reshape(1, D), "g_xat": f(norm_xattn_g).reshape(1, D),
        "g_mem": f(norm_mem_g).reshape(1, D), "g_ffn": f(norm_ffn_g).reshape(1, D),
        "g_fin": f(norm_final_g).reshape(1, D),
        "w_in": f(w_in)[0], "w_pa": f(w_branch_a)[0], "w_pb": f(w_branch_b)[0], "w_out": f(w_out)[0],
        "w_xq": f(w_xq)[0], "w_xkv": f(w_xkv)[0], "w_xo": f(w_xo)[0],
        "w_fg": f(w_ff_gate)[0], "w_fu": f(w_ff_up)[0], "w_fd": f(w_ff_down)[0],
        "ln_g": f(sgu_ln_g).reshape(1, 2048), "ln_b": f(sgu_ln_b).reshape(1, 2048),
        "w_sT": np.ascontiguousarray(np.transpose(f(w_sgu)[0], (2, 0, 1))).reshape(128, 16 * 128),
        "b_s": f(b_sgu)[0].reshape(1, 16 * 128),
        "tri": tri, "ident": ident, "perm": perm,
        "cos_all": cosT, "sin_all": sinT,
    }
    in_maps = []
    own_idx = []
    for c in range(8):
        b, j = c // 4, c % 4
        blocks = [4 * i + j for i in range(8)]
        tok = np.concatenate([np.arange(g * BLK, (g + 1) * BLK) for g in blocks])
        own_idx.append((b, tok))
        negb = np.zeros((8, 32), np.float32)
        validm = np.zeros((8, 32), np.float32)
        ownm = np.zeros((8, 32), np.float32)
        for i in range(8):
            cur = 4 * i + j
            negb[i, cur:] = NEG
            validm[i, :cur] = 1.0
            ownm[i, cur] = 1.0
        dm = np.zeros((128, 4, 2, 256), np.float32)
        kk = np.arange(128)
        q = np.arange(256)
        for cc in range(4):
            for ch in range(2):
                if cc < j:
                    dm[:, cc, ch, :] = 1.0
                elif cc == j:
                    dm[:, cc, ch, :] = ((ch * 128 + kk)[:, None] <= q[None, :]).astype(np.float32)
        m = dict(shared)
        m.update({
            "x_own": np.ascontiguousarray(x[b][tok]),
            "x_all": x[b],
            "mem_b": mem[b],
            "cos_own": np.ascontiguousarray(cosT[:, tok]),
            "sin_own": np.ascontiguousarray(sinT[:, tok]),
            "negb": negb.reshape(1, 256), "validm": validm.reshape(1, 256), "ownm": ownm.reshape(1, 256),
            "dmask": dm.reshape(128, 4 * 512),
        })
        in_maps.append(m)
    if _PROG is None:
        _PROG = build_program()
    res = run_bass_kernel_spmd(_PROG, in_maps, core_ids=list(range(8)))
    outp = np.empty((NB, SEQ, D), np.float32)
    for c in range(8):
        b, tok = own_idx[c]
        outp[b, tok] = res.results[c]["out"]
    return outp
```
